# Optimizing a Trainium2 kernel written in Bass

```python
import jax, jax.numpy as jnp
from jax import lax
import numpy as np

D_MODEL = 1024
BATCH = 16
SEQ = 256
DEPTH = 4
DEC_BATCH = 2
DEC_SEQ = 1024
PAST_LEN = 512

GRID_W = 64
NA_WIDTH = D_MODEL // 2
NA_HEAD_DIM = 64
NA_HEADS = NA_WIDTH // NA_HEAD_DIM
NA_WIN_ROWS = 8
NA_WIN_COLS = 16
GLA_V_WIDTH = D_MODEL - NA_WIDTH
GLA_HEADS = 4
GLA_DV = GLA_V_WIDTH // GLA_HEADS
GLA_DK = GLA_DV // 2
GLA_K_WIDTH = GLA_HEADS * GLA_DK
GLA_GATE_RANK = 16
GLA_GATE_TAU = 16.0
GLA_CHUNK = 64
D_FF = 4 * D_MODEL
ROPE_BASE = 10000.0
ATTN_QBLOCK = 128
N_MOD = 6
EPS = 1e-6
SPLIT_SIZES = [NA_WIDTH] * 3 + [GLA_K_WIDTH] * 2 + [GLA_V_WIDTH] * 2 + [GLA_GATE_RANK] * 2
SPLITS = np.cumsum(SPLIT_SIZES)[:-1].tolist()
IN_WIDTH = int(sum(SPLIT_SIZES))

kernel_name = "hybrid_na_gla_diffusion_step"


def rmsnorm(x, g):
    xf = x.astype(jnp.float32)
    y = xf * lax.rsqrt(jnp.mean(xf * xf, axis=-1, keepdims=True) + EPS)
    return (y * g.astype(jnp.float32)).astype(x.dtype)


def heads(t, n):
    B, L, _ = t.shape
    return t.reshape(B, L, n, -1).transpose(0, 2, 1, 3)


def ada(cvec, w, b):
    m = jax.nn.silu(cvec) @ w + b
    return jnp.split(m, N_MOD, axis=-1)


def axial_rope(x):
    L, d = x.shape[2], x.shape[3]
    half = d // 2
    nf = half // 2
    inv = ROPE_BASE ** (-jnp.arange(nf, dtype=jnp.float32) / nf)
    t = jnp.arange(L)
    ang_r = (t // GRID_W).astype(jnp.float32)[:, None] * inv
    ang_c = (t % GRID_W).astype(jnp.float32)[:, None] * inv

    def rot(xh, ang):
        cos = jnp.cos(ang).astype(x.dtype)
        sin = jnp.sin(ang).astype(x.dtype)
        x1, x2 = xh[..., :nf], xh[..., nf:]
        return jnp.concatenate([x1 * cos - x2 * sin, x1 * sin + x2 * cos], axis=-1)

    return jnp.concatenate([rot(x[..., :half], ang_r), rot(x[..., half:], ang_c)], axis=-1)


def project(h, w_in, g_q, g_k, w_gf, b_gf, w_gb, b_gb):
    z = h @ w_in
    q, k, v, gq, gk, gv, gate, zf, zb = jnp.split(z, SPLITS, axis=-1)
    q = rmsnorm(heads(q, NA_HEADS), g_q)
    k = rmsnorm(heads(k, NA_HEADS), g_k)
    v = heads(v, NA_HEADS)
    gq = heads(gq, GLA_HEADS)
    gk = heads(gk, GLA_HEADS)
    gv = heads(gv, GLA_HEADS)
    lf = heads(jax.nn.log_sigmoid((zf @ w_gf + b_gf).astype(jnp.float32)) / GLA_GATE_TAU, GLA_HEADS)
    lb = heads(jax.nn.log_sigmoid((zb @ w_gb + b_gb).astype(jnp.float32)) / GLA_GATE_TAU, GLA_HEADS)
    return q, k, v, gq, gk, gv, gate, lf, lb


def gla_chunked(q, k, v, logf, s0):
    B, H, L, dk = q.shape
    dv = v.shape[-1]
    n = L // GLA_CHUNK

    def to_chunks(t):
        return jnp.moveaxis(t.astype(jnp.float32).reshape(B, H, n, GLA_CHUNK, t.shape[-1]), 2, 0)

    causal = jnp.tril(jnp.ones((GLA_CHUNK, GLA_CHUNK), dtype=bool))[..., None]

    def step(S, inp):
        qi, ki, vi, fi = inp
        b = jnp.cumsum(fi, axis=-2)
        diff = b[..., :, None, :] - b[..., None, :, :]
        decay = jnp.exp(jnp.where(causal, diff, -jnp.inf))
        attn = jnp.einsum('bhid,bhjd,bhijd->bhij', qi, ki, decay)
        o = jnp.einsum('bhij,bhje->bhie', attn, vi) + jnp.einsum('bhid,bhde->bhie', qi * jnp.exp(b), S)
        b_last = b[..., -1:, :]
        S = jnp.exp(b_last)[..., 0, :, None] * S + jnp.einsum('bhjd,bhje->bhde', ki * jnp.exp(b_last - b), vi)
        return S, o

    S, o = lax.scan(step, s0.astype(jnp.float32), (to_chunks(q), to_chunks(k), to_chunks(v), to_chunks(logf)))
    o = jnp.moveaxis(o, 0, 2).reshape(B, H, L, dv)
    return o, S


def gla_bidir(q, k, v, lf, lb, sf0, sb0):
    q = q * (GLA_DK ** -0.5)
    of, sf = gla_chunked(q, k, v, lf, sf0)
    flip = lambda t: jnp.flip(t, axis=2)
    ob, sb = gla_chunked(flip(q), flip(k), flip(v), flip(lb), sb0)
    return (of + flip(ob)).astype(v.dtype), sf, sb


def context_attention(q, k, v):
    B, H, L, d = q.shape
    nb = L // ATTN_QBLOCK
    qb = jnp.moveaxis(q.reshape(B, H, nb, ATTN_QBLOCK, d), 2, 0)
    scale = d ** -0.5

    def blk(qi):
        s = jnp.einsum('bhqd,bhkd->bhqk', qi, k).astype(jnp.float32) * scale
        p = jax.nn.softmax(s, axis=-1).astype(v.dtype)
        return jnp.einsum('bhqk,bhkd->bhqd', p, v)

    o = lax.map(blk, qb)
    return jnp.moveaxis(o, 0, 2).reshape(B, H, L, d)


def neighbourhood_attention(q, k, v, kc, vc, rpb):
    B, H, N, d = q.shape
    rows = N // GRID_W
    wr = min(NA_WIN_ROWS, rows)
    wc = NA_WIN_COLS
    nw = wr * wc
    cols = np.arange(GRID_W)
    cs = np.clip(cols - wc // 2, 0, GRID_W - wc)
    col_idx = cs[:, None] + np.arange(wc)[None, :]
    dc = col_idx - cols[:, None] + (NA_WIN_COLS - 1)
    kg = k.reshape(B, H, rows, GRID_W, d)
    vg = v.reshape(B, H, rows, GRID_W, d)
    qg = q.reshape(B, H, rows, GRID_W, d)
    scale = d ** -0.5

    def row_fn(args):
        r, q_row = args
        rs = jnp.clip(r - wr // 2, 0, rows - wr)
        k_rows = lax.dynamic_slice_in_dim(kg, rs, wr, axis=2)
        v_rows = lax.dynamic_slice_in_dim(vg, rs, wr, axis=2)
        k_win = k_rows[:, :, :, col_idx].transpose(0, 1, 3, 2, 4, 5).reshape(B, H, GRID_W, nw, d)
        v_win = v_rows[:, :, :, col_idx].transpose(0, 1, 3, 2, 4, 5).reshape(B, H, GRID_W, nw, d)
        dr = rs + jnp.arange(wr) - r + (NA_WIN_ROWS - 1)
        bias = rpb[:, dr][:, :, dc].transpose(0, 2, 1, 3).reshape(H, GRID_W, nw)
        s_win = jnp.einsum('bhcd,bhckd->bhck', q_row, k_win).astype(jnp.float32) * scale + bias.astype(jnp.float32)
        s_ctx = jnp.einsum('bhcd,bhld->bhcl', q_row, kc).astype(jnp.float32) * scale
        p = jax.nn.softmax(jnp.concatenate([s_win, s_ctx], axis=-1), axis=-1).astype(v.dtype)
        return (jnp.einsum('bhck,bhckd->bhcd', p[..., :nw], v_win)
                + jnp.einsum('bhcl,bhld->bhcd', p[..., nw:], vc))

    o = lax.map(row_fn, (jnp.arange(rows), jnp.moveaxis(qg, 2, 0)))
    return jnp.moveaxis(o, 0, 2).reshape(B, H, N, d)


def merge(o_na, o_gla, gate, g_out, w_o):
    B, _, L, _ = o_na.shape
    a = o_na.transpose(0, 2, 1, 3).reshape(B, L, NA_WIDTH)
    g = rmsnorm(o_gla, g_out).transpose(0, 2, 1, 3).reshape(B, L, GLA_V_WIDTH) * jax.nn.silu(gate)
    return jnp.concatenate([a, g], axis=-1) @ w_o


def mlp(h, w_up, w_down):
    return jnp.square(jax.nn.relu(h @ w_up)) @ w_down


def setup_inputs(seed: int = 0) -> dict:
    key = jax.random.key(seed)
    ks = jax.random.split(key, 24)
    nrm = lambda k, s: jax.random.normal(k, s, dtype=jnp.float32)
    D = D_MODEL
    return {
        "x_prompt": nrm(ks[0], (BATCH, SEQ, D)),
        "x_sample": nrm(ks[1], (DEC_BATCH, DEC_SEQ, D)),
        "cache_k": nrm(ks[2], (DEC_BATCH, DEPTH, NA_HEADS, PAST_LEN, NA_HEAD_DIM)),
        "cache_v": nrm(ks[3], (DEC_BATCH, DEPTH, NA_HEADS, PAST_LEN, NA_HEAD_DIM)),
        "state_fwd": nrm(ks[4], (DEC_BATCH, DEPTH, GLA_HEADS, GLA_DK, GLA_DV)),
        "state_bwd": nrm(ks[5], (DEC_BATCH, DEPTH, GLA_HEADS, GLA_DK, GLA_DV)),
        "c": nrm(ks[6], (DEC_BATCH, D)),
        "c_ctx": nrm(ks[7], (D,)),
        "w_ada": nrm(ks[8], (DEPTH, D, N_MOD * D)) * (0.5 * D ** -0.5),
        "b_ada": nrm(ks[9], (DEPTH, N_MOD * D)) * 0.02,
        "g_attn": 1.0 + 0.02 * nrm(ks[10], (DEPTH, D)),
        "w_in": nrm(ks[11], (DEPTH, D, IN_WIDTH)) * D ** -0.5,
        "g_q": 1.0 + 0.02 * nrm(ks[12], (DEPTH, NA_HEAD_DIM)),
        "g_k": 1.0 + 0.02 * nrm(ks[13], (DEPTH, NA_HEAD_DIM)),
        "rpb": 0.1 * nrm(ks[14], (DEPTH, NA_HEADS, 2 * NA_WIN_ROWS - 1, 2 * NA_WIN_COLS - 1)),
        "w_gf": nrm(ks[15], (DEPTH, GLA_GATE_RANK, GLA_K_WIDTH)) * GLA_GATE_RANK ** -0.5,
        "b_gf": 0.1 * nrm(ks[16], (DEPTH, GLA_K_WIDTH)),
        "w_gb": nrm(ks[17], (DEPTH, GLA_GATE_RANK, GLA_K_WIDTH)) * GLA_GATE_RANK ** -0.5,
        "b_gb": 0.1 * nrm(ks[18], (DEPTH, GLA_K_WIDTH)),
        "g_gla_out": 1.0 + 0.02 * nrm(ks[19], (DEPTH, GLA_DV)),
        "w_o": nrm(ks[20], (DEPTH, D, D)) * D ** -0.5,
        "g_mlp": 1.0 + 0.02 * nrm(ks[21], (DEPTH, D)),
        "w_up": nrm(ks[22], (DEPTH, D, D_FF)) * D ** -0.5,
        "w_down": nrm(ks[23], (DEPTH, D_FF, D)) * D_FF ** -0.5,
    }


def reference(x_prompt, x_sample, cache_k, cache_v, state_fwd, state_bwd, c, c_ctx,
              w_ada, b_ada, g_attn, w_in, g_q, g_k, rpb, w_gf, b_gf, w_gb, b_gb,
              g_gla_out, w_o, g_mlp, w_up, w_down):
    y = x_prompt
    nb = x_prompt.shape[0]
    ks_, vs_, sfs, sbs = [], [], [], []
    for l in range(DEPTH):
        sh1, sc1, ga1, sh2, sc2, ga2 = ada(c_ctx, w_ada[l], b_ada[l])
        h = rmsnorm(y, g_attn[l]) * (1 + sc1) + sh1
        q, k, v, gq, gk, gv, gate, lf, lb = project(h, w_in[l], g_q[l], g_k[l], w_gf[l], b_gf[l], w_gb[l], b_gb[l])
        o_na = context_attention(q, k, v)
        zeros = jnp.zeros((nb, GLA_HEADS, GLA_DK, GLA_DV), jnp.float32)
        o_gla, sf, sb = gla_bidir(gq, gk, gv, lf, lb, zeros, zeros)
        y = y + ga1 * merge(o_na, o_gla, gate, g_gla_out[l], w_o[l])
        h = rmsnorm(y, g_mlp[l]) * (1 + sc2) + sh2
        y = y + ga2 * mlp(h, w_up[l], w_down[l])
        ks_.append(k)
        vs_.append(v)
        sfs.append(sf)
        sbs.append(sb)
    new_k = jnp.stack(ks_, axis=1)
    new_v = jnp.stack(vs_, axis=1)
    new_sf = jnp.stack(sfs, axis=1)
    new_sb = jnp.stack(sbs, axis=1)

    x = x_sample
    for l in range(DEPTH):
        sh1, sc1, ga1, sh2, sc2, ga2 = [m[:, None, :] for m in ada(c, w_ada[l], b_ada[l])]
        h = rmsnorm(x, g_attn[l]) * (1 + sc1) + sh1
        q, k, v, gq, gk, gv, gate, lf, lb = project(h, w_in[l], g_q[l], g_k[l], w_gf[l], b_gf[l], w_gb[l], b_gb[l])
        gq = axial_rope(gq)
        gk = axial_rope(gk)
        o_na = neighbourhood_attention(q, k, v, cache_k[:, l], cache_v[:, l], rpb[l])
        o_gla, _, _ = gla_bidir(gq, gk, gv, lf, lb, state_fwd[:, l], state_bwd[:, l])
        x = x + ga1 * merge(o_na, o_gla, gate, g_gla_out[l], w_o[l])
        h = rmsnorm(x, g_mlp[l]) * (1 + sc2) + sh2
        x = x + ga2 * mlp(h, w_up[l], w_down[l])

    return (y, x, new_k, new_v, new_sf, new_sb)
```

```python
import numpy as np
from contextlib import ExitStack
import concourse.bass as bass
import concourse.mybir as mybir
from concourse.bass_utils import run_bass_kernel_spmd

F32 = mybir.dt.float32
BF16 = mybir.dt.bfloat16
AF = mybir.ActivationFunctionType
ALU = mybir.AluOpType

SAME_ENGINE_SYNC = True
SAME_ENGINE_FULL = True
NSLAB = 3
NEG = -30000.0


class Tile:
    __slots__ = ("name", "t", "w", "r", "excl")

    def __init__(self, name, t=None, excl=False):
        self.name = name
        self.t = t
        self.w = {}
        self.r = []
        self.excl = excl

    def __getitem__(self, idx):
        return self.t[idx]


class Op:
    __slots__ = ("queue", "clock", "ordv", "fn", "deps", "waits", "has_waiter", "snap", "semval", "gi", "nraw")

    def __init__(self, queue, clock, fn):
        self.queue = queue
        self.clock = clock
        self.fn = fn
        self.deps = []
        self.waits = []
        self.has_waiter = False
        self.snap = None
        self.semval = None


class Prog:
    QUEUES = ("pe", "act", "dve", "pool", "sp")

    def __init__(self):
        self.ops = []
        self.clock_n = {}
        self.last_on_clock = {}
        self.stores = []

    def add(self, queue, fn, reads=(), writes=(), key=None, inc=16):
        clock = queue if key is None else ("dma", key, inc)
        op = Op(queue, clock, fn)
        n = self.clock_n.get(clock, 0) + 1
        self.clock_n[clock] = n
        op.ordv = n
        op.gi = len(self.ops)
        deps = []
        if key is not None:
            prev = self.last_on_clock.get(clock)
            if prev is not None:
                deps.append(prev)
        self.last_on_clock[clock] = op
        ex = [t for t in reads if t.excl]
        if ex:
            reads = [t for t in reads if not t.excl]
            writes = list(writes) + [t for t in ex if t not in writes]
        for t in reads:
            deps.extend(t.w.values())
        nraw = len(deps)
        for t in writes:
            deps.extend(t.w.values())
            deps.extend(t.r)
        op.nraw = nraw
        for t in reads:
            t.r.append(op)
        for t in writes:
            t.w[clock] = op
            t.r = []
        op.deps = deps
        self.ops.append(op)
        return op

    def dma(self, queue, out, in_, reads=(), writes=(), key=None, store=False):
        if key is None:
            key = (writes[0].name if writes else reads[0].name + "_st")

        def fn(eng, out=out, in_=in_):
            return eng.dma_start(out=out, in_=in_)
        op = self.add(queue, fn, reads=reads, writes=writes, key=key)
        if store:
            self.stores.append(op)
        return op

    def finalize(self):
        known = {q: {} for q in self.QUEUES}
        for op in self.ops:
            kq = known[op.queue]
            need = {}
            for di, d in enumerate(op.deps):
                if d.clock == op.clock:
                    if isinstance(op.clock, tuple):
                        pass
                    elif op.queue == "pe" or not SAME_ENGINE_SYNC:
                        continue
                    elif di >= op.nraw and not SAME_ENGINE_FULL:
                        continue
                cur = need.get(d.clock)
                if cur is None or cur.ordv < d.ordv:
                    need[d.clock] = d
            for d in sorted(need.values(), key=lambda d: -d.gi):
                if kq.get(d.clock, 0) >= d.ordv:
                    continue
                op.waits.append(d)
                d.has_waiter = True
                for c2, v2 in d.snap.items():
                    if kq.get(c2, 0) < v2:
                        kq[c2] = v2
                if kq.get(d.clock, 0) < d.ordv:
                    kq[d.clock] = d.ordv
            snap = dict(kq)
            if snap.get(op.clock, 0) < op.ordv:
                snap[op.clock] = op.ordv
            op.snap = snap
        cnt = {}
        for op in self.ops:
            if isinstance(op.clock, tuple):
                op.semval = op.clock[2] * op.ordv
            elif op.has_waiter:
                cnt[op.clock] = cnt.get(op.clock, 0) + 1
                op.semval = cnt[op.clock]
        for op in self.ops:
            op.snap = None

    def emit(self, nc, final_wait_queue="sp"):
        self.finalize()
        clocks = []
        seen = set()
        for op in self.ops:
            if op.clock not in seen and (isinstance(op.clock, tuple) or op.has_waiter):
                seen.add(op.clock)
                clocks.append(op.clock)
        with ExitStack() as es:
            sems = {}
            for i, c in enumerate(clocks):
                sems[c] = es.enter_context(nc.semaphore("s%d" % i))
            block = es.enter_context(nc.Block())
            byq = {q: [o for o in self.ops if o.queue == q] for q in self.QUEUES}
            stores = self.stores

            def run(eng, q):
                for op in byq[q]:
                    for d in op.waits:
                        eng.wait_ge(sems[d.clock], d.semval)
                    ins = op.fn(eng)
                    if isinstance(op.clock, tuple):
                        ins.then_inc(sems[op.clock], op.clock[2])
                    elif op.has_waiter:
                        ins.then_inc(sems[op.clock], 1)
                if q == final_wait_queue:
                    last = {}
                    for s in stores:
                        last[s.clock] = max(last.get(s.clock, 0), s.semval)
                    for c, v in last.items():
                        eng.wait_ge(sems[c], v)

            @block.tensor
            def _(e):
                run(e, "pe")

            @block.scalar
            def _(e):
                run(e, "act")

            @block.vector
            def _(e):
                run(e, "dve")

            @block.gpsimd
            def _(e):
                run(e, "pool")

            @block.sync
            def _(e):
                run(e, "sp")
        self.stats = "ops=%d %s waits=%d sems=%d" % (
            len(self.ops), {q: len(v) for q, v in byq.items()},
            sum(len(o.waits) for o in self.ops), len(clocks))


class Rot:
    def __init__(self, tiles):
        self.tiles = tiles
        self.i = 0
        self.held = set()

    def next(self):
        for _ in range(len(self.tiles)):
            t = self.tiles[self.i % len(self.tiles)]
            self.i += 1
            if t.name not in self.held:
                return t
        raise RuntimeError("pool exhausted: " + self.tiles[0].name)

    def hold(self):
        t = self.next()
        self.held.add(t.name)
        return t

    def release(self, t):
        self.held.discard(t.name)


def _dtsize(dt):
    return 2 if dt == BF16 else 4


class Arena:
    def __init__(self, nc):
        self.nc = nc
        self.off = 16512
        self.top = 229344
        self.peak = 0
        self.n = 0

    def alloc(self, name, shape, dt):
        nbytes = int(np.prod(shape[1:])) * _dtsize(dt)
        off = (self.off + 31) // 32 * 32
        if off + nbytes > self.top:
            raise RuntimeError("SBUF arena overflow at %s: need %d > %d" % (name, off + nbytes, self.top))
        self.n += 1
        t = self.nc.alloc_sbuf_tensor_at("%s_%d" % (name, self.n), list(shape), dt, offset=off)
        self.off = off + nbytes
        self.peak = max(self.peak, self.off)
        return t


C_ID, C_BLK, C_ROT, C_TRIU, C_TRIL, C_COS, C_SIN, C_OH = 0, 128, 256, 384, 512, 640, 896, 1152
NCST = 1156
G_ATT, G_MLP, G_Q, G_K, G_OUT, G_BF, G_BB = 0, 32, 64, 68, 72, 76, 84
NGV = 92


def build_program(n_layers=4):
    nc = bass.Bass("TRN2", target_bir_lowering=False)
    P = Prog()
    ar = Arena(nc)

    def din(name, shape, dt=F32):
        return nc.dram_tensor(name, list(shape), dt, kind="ExternalInput").ap()

    def dout(name, shape, dt=F32):
        return nc.dram_tensor(name, list(shape), dt, kind="ExternalOutput").ap()

    xT_d = din("xT", [128, 8, 768])
    cT_d = din("cT", [128, 8, 2])
    w_ada_d = din("w_ada", [4, 1024, 6144])
    w_in_d = din("w_in", [4, 1024, 3104])
    w_o_d = din("w_o", [4, 1024, 1024])
    w_up_d = din("w_up", [4, 1024, 4096])
    w_down_d = din("w_down", [4, 4096, 1024])
    badaT_d = din("badaT", [128, 4, 48])
    gvec_d = din("gvec", [128, NGV])
    wg_d = din("wg", [32, 4, 512])
    ckT_d = din("ckT", [4, 128, 4, 520])
    cv_d = din("cv", [4, 128, 4, 520])
    stf_d = din("stf", [4, 128, 2, 128])
    stb_d = din("stb", [4, 128, 2, 128])
    mask_d = din("mask", [4, 8, 128, 2048])
    cst_d = din("cst", [128, NCST])

    yT_d = dout("yT", [128, 8, 768])
    nkT_d = dout("nkT", [4, 128, 4, 512])
    nv_d = dout("nv", [4, 128, 4, 512])
    nsf_d = dout("nsf", [4, 2, 128, 2, 128])
    nsb_d = dout("nsb", [4, 2, 128, 2, 128])

    cc1_in = nc.dram_tensor("cc1_in", [128, 2048], BF16)
    cc1_out = nc.dram_tensor("cc1_out", [512, 2048], BF16)
    cc2_in = nc.dram_tensor("cc2_in", [128, 520], F32)
    cc2_out = nc.dram_tensor("cc2_out", [512, 520], F32)
    CC1I, CC1O, CC2I, CC2O = Tile("cc1i"), Tile("cc1o"), Tile("cc2i"), Tile("cc2o")

    banks = [nc.alloc_psum_tensor("pb%d" % i, [128, 512], F32) for i in range(8)]
    MM = Rot([Tile("mm0", banks[0][:, :], True), Tile("mm1", banks[1][:, :], True),
              Tile("st0", banks[2][:, :], True), Tile("st1", banks[3][:, :], True)])
    SSA = Tile("ssA", banks[6][:, 0:256], True)
    MODP_AP = banks[6][:, 256:352]
    ST = MM
    OD = Rot([Tile("od0", banks[4][:, :], True), Tile("od1", banks[5][:, :], True), Tile("od2", banks[7][:, :], True)])

    def tl(name, shape, dt):
        t = ar.alloc(name, shape, dt)
        return Tile(name, t[:] if len(shape) == 2 else t)

    x_t = ar.alloc("x", [128, 8, 768], F32)
    X = [Tile("x%d" % k, x_t[:, k, :]) for k in range(8)]
    h_t = ar.alloc("h", [128, 8, 768], BF16)
    H = [Tile("h%d" % k, h_t[:, k, :]) for k in range(8)]
    SL = Rot([tl("slab%d" % i, [128, 4160], BF16) for i in range(NSLAB)])
    WZ = tl("wz", [128, 8, 34], BF16)
    CST = tl("cst", [128, NCST], F32)
    GVEC = tl("gvec", [128, NGV], F32)
    NEGB = tl("negb", [128, 16], F32)
    BADA = tl("bada", [128, 4, 48], F32)
    WG = tl("wg", [32, 512], F32)
    CTS = tl("cts", [128, 8, 2], F32)
    SCB = tl("scb", [128, 8, 2], BF16)
    MOD = [tl("mod%d" % i, [128, 48, 2], F32) for i in range(2)]
    A1 = tl("A1", [128, 8, 2], F32)
    A2 = tl("A2", [128, 8, 2], F32)
    BLKB = tl("blkb", [128, 128], BF16)
    ONEB = tl("oneb", [128, 128], BF16)
    TRI4 = tl("tri4", [128, 512], BF16)
    ONEF = tl("onef", [128, 256], F32)
    COLS = tl("cols", [128, 4], F32)
    SQ = Rot([tl("sq%d" % i, [128, 256], BF16) for i in range(3)])
    STD = Rot([tl("std%d" % i, [128, 256], F32) for i in range(2)])
    TMPF = Rot([tl("tmpf%d" % i, [128, 256], F32) for i in range(4)])

    RL = Rot([tl("rl%d" % i, [128, 512], BF16) for i in range(2)])
    QP = [Rot([tl("qp%d_%d" % (hh_, i), [128, 256], BF16) for i in range(2)]) for hh_ in range(2)]

    class GSet:
        pass

    def mk_gset(nm):
        s = GSet()
        s.name = nm
        s.qt = [[tl("%sq%d%d" % (nm, d, c), [128, 128], BF16) for c in range(2)] for d in range(2)]
        s.kt = [[tl("%sk%d%d" % (nm, d, c), [128, 128], BF16) for c in range(2)] for d in range(2)]
        s.e = [[tl("%se%d%d" % (nm, d, c), [128, 1], F32) for c in range(2)] for d in range(2)]
        s.mid = [tl("%sm%d" % (nm, d), [128, 128], F32) for d in range(2)]
        s.fin = [tl("%sf%d" % (nm, d), [128, 128], F32) for d in range(2)]
        s.midb = [tl("%sn%d" % (nm, d), [128, 128], BF16) for d in range(2)]
        s.inib = [tl("%si%d" % (nm, d), [128, 128], BF16) for d in range(2)]
        return s

    class KSet:
        pass

    def mk_kset(nm):
        s = KSet()
        s.name = nm
        s.km0 = [[tl("%sa%d%d" % (nm, d, c), [128, 128], BF16) for c in range(2)] for d in range(2)]
        s.km1 = [[tl("%sb%d%d" % (nm, d, c), [128, 128], BF16) for c in range(2)] for d in range(2)]
        return s

    GS_P = Rot([mk_gset("gpa")])
    GS_S = [mk_gset("gs0"), mk_gset("gs1")]
    KMS = Rot([mk_kset("kma"), mk_kset("kmb")])

    mark = ar.off
    qn_t = ar.alloc("qn", [128, 4, 768], BF16)
    QN = [Tile("qn%d" % i, qn_t[:, i, :]) for i in range(4)]
    kn_t = ar.alloc("kn", [128, 4, 768], BF16)
    KN = [Tile("kn%d" % i, kn_t[:, i, :]) for i in range(4)]
    F512 = Rot([tl("f512_%d" % i, [128, 512], F32) for i in range(8)])
    KNF = F512
    GQ = [tl("gq%d" % i, [128, 768], BF16) for i in range(2)]
    GK = [tl("gk%d" % i, [128, 768], BF16) for i in range(2)]
    SG = [tl("sg%d" % i, [128, 768], BF16) for i in range(4)]
    VOWN = [tl("vown%d" % i, [128, 512], BF16) for i in range(4)]
    VS = tl("vs", [128, 2, 512], BF16)
    VST = F512
    GV = [tl("gv%d" % i, [128, 512], BF16) for i in range(6)]
    ZF = tl("zf", [32, 768], F32)
    SP = [tl("sp%d" % i, [128, 256], F32) for i in range(4)]
    CS = [tl("cs%d" % i, [128, 256], F32) for i in range(4)]
    GP = Rot([tl("gp%d" % i, [128, 4, 512], BF16) for i in range(2)])
    G2 = tl("g2", [128, 4, 520], F32)
    STG2 = tl("stg2", [128, 520], F32)
    CKB = tl("ckb", [128, 4, 520], BF16)
    CVB = tl("cvb", [128, 4, 520], BF16)
    STIN = [tl("stinf", [128, 2, 128], F32), tl("stinb", [128, 2, 128], F32)]
    MK = F512
    PT = Rot([tl("pt%d" % i, [128, 512], BF16) for i in range(3)])
    RD = Rot([tl("rd%d" % i, [128, 256], F32) for i in range(2)])
    RAW = F512
    SQB = Rot([tl("sqb%d" % i, [128, 512], BF16) for i in range(3)])
    STD5 = F512
    EX = Rot([tl("ex%d" % i, [128, 128], F32) for i in range(4)])
    KT32 = Rot([tl("kt32%d" % i, [128, 128], F32) for i in range(2)])
    AM = SQB
    BIA = Rot([tl("bia%d" % i, [128, 2], F32) for i in range(4)])

    TMPS = Rot([tl("tmps%d" % i, [128, 128], F32) for i in range(2)])
    att_end = ar.off
    ar.off = mark
    h1_t = [ar.alloc("h1_%d" % s, [128, 4, 768], BF16) for s in range(8)]
    H1 = [Tile("h1_%d" % s, h1_t[s]) for s in range(8)]
    ar.off = max(att_end, ar.off)
    DUM = tl("dum", [128, 8], F32)

    att_tiles = (QN + KN + F512.tiles + GQ + GK + SG + VOWN + [VS] + GV + [ZF] + SP + CS +
                 GP.tiles + [G2, STG2, CKB, CVB] + STIN + PT.tiles + RD.tiles +
                 SQB.tiles + EX.tiles + KT32.tiles + BIA.tiles +
                 TMPS.tiles)

    def act(out, in_, func, reads, writes, bias=None, scale=None):
        kw = {}
        if bias is not None:
            kw["bias"] = bias
        if scale is not None:
            kw["scale"] = scale
        P.add("act", lambda e: e.activation(out=out, in_=in_, func=func, **kw), reads=reads, writes=writes)

    def tt(out, in0, in1, op, reads, writes, eng="dve"):
        P.add(eng, lambda e: e.tensor_tensor(out=out, in0=in0, in1=in1, op=op), reads=reads, writes=writes)

    def ts(out, in0, s1, op0, reads, writes, s2=None, op1=None, eng="dve"):
        if op1 is None:
            P.add(eng, lambda e: e.tensor_scalar(out=out, in0=in0, scalar1=s1, scalar2=None, op0=op0),
                  reads=reads, writes=writes)
        else:
            P.add(eng, lambda e: e.tensor_scalar(out=out, in0=in0, scalar1=s1, scalar2=s2, op0=op0, op1=op1),
                  reads=reads, writes=writes)

    def stt(out, in0, scalar, in1, op0, op1, reads, writes):
        P.add("dve", lambda e: e.scalar_tensor_tensor(out=out, in0=in0, scalar=scalar, in1=in1, op0=op0, op1=op1),
              reads=reads, writes=writes)

    def cp(out, in_, reads, writes, eng="dve"):
        if eng == "act":
            act(out, in_, AF.Copy, reads, writes)
        else:
            P.add(eng, lambda e: e.tensor_copy(out=out, in_=in_), reads=reads, writes=writes)

    def recip(out, in_, reads, writes):
        P.add("dve", lambda e: e.reciprocal(out=out, in_=in_), reads=reads, writes=writes)

    def mm(out, lhsT, rhs, start, stop, reads, writes, skip=False):
        if skip:
            P.add("pe", lambda e: e.matmul(out, lhsT=lhsT, rhs=rhs, start=start, stop=stop, skip_group_check=True),
                  reads=reads, writes=writes)
        else:
            P.add("pe", lambda e: e.matmul(out, lhsT=lhsT, rhs=rhs, start=start, stop=stop), reads=reads, writes=writes)

    def rsqrt_act(T, dst_ap, src_ap, src_tiles, scale):
        act(dst_ap, src_ap, AF.Ln, src_tiles + [COLS], [T], bias=EPS, scale=scale)
        act(dst_ap, dst_ap, AF.Exp, [T], [T], scale=-0.5)

    def run_pool(jobs, k):
        jobs = iter(jobs)
        active = []
        while True:
            while len(active) < k:
                j = next(jobs, None)
                if j is None:
                    break
                active.append(j)
            if not active:
                break
            for g in list(active):
                try:
                    next(g)
                except StopIteration:
                    active.remove(g)

    def memset(ap, val, writes, eng="dve"):
        P.add(eng, lambda e: e.memset(ap, val), writes=writes)

    EPS = COLS.t[:, 0:1]
    ONE = COLS.t[:, 1:2]
    ZERO = COLS.t[:, 2:3]
    IDF = CST.t[:, C_ID:C_ID + 128]
    ROTF = CST.t[:, C_ROT:C_ROT + 128]
    COS = CST.t[:, C_COS:C_COS + 256]
    SIN = CST.t[:, C_SIN:C_SIN + 256]

    tiles_tok = [(0, 512, 0), (512, 256, 1)]

    P.dma("sp", x_t[:], xT_d, writes=X, key="xload")
    P.dma("sp", CST.t, cst_d, writes=[CST])
    P.dma("sp", GVEC.t, gvec_d, writes=[GVEC])
    P.dma("sp", BADA.t[:], badaT_d, writes=[BADA])
    P.dma("sp", CTS.t[:], cT_d, writes=[CTS])
    memset(COLS.t[:, 0:1], 1e-6, [COLS])
    memset(COLS.t[:, 1:2], 1.0, [COLS])
    memset(COLS.t[:, 2:3], 0.0, [COLS])
    memset(ONEB.t, 1.0, [ONEB])
    memset(ONEF.t, 1.0, [ONEF])
    cp(BLKB.t, CST.t[:, C_BLK:C_BLK + 128], [CST], [BLKB])
    for q_ in range(2):
        cp(TRI4.t[:, 256 * q_:256 * q_ + 256], CST.t[:, C_TRIU:C_TRIU + 256], [CST], [TRI4])
    ts(NEGB.t, GVEC.t[:, G_BF:G_BF + 16], -1.0, ALU.mult, [GVEC], [NEGB])
    act(SCB.t[:], CTS.t[:], AF.Silu, [CTS], [SCB])
    for hh_ in range(2):
        for t_ in QP[hh_].tiles:
            memset(t_.t, 0.0, [t_])
    for ks in KMS.tiles:
        for d in range(2):
            for c in range(2):
                memset(ks.km0[d][c].t, 0.0, [ks.km0[d][c]])
                memset(ks.km1[d][c].t, 0.0, [ks.km1[d][c]])

    def load_slab(src, a, b):
        sl = SL.next()
        bp = 4160 // a
        view = sl.t[:, 0:a * bp].rearrange("p (a b) -> p a b", b=bp)[:, :, 0:b]
        P.dma("pool", view, src, writes=[sl])
        return sl, view

    def kview(w_l, c0, c1):
        return w_l.rearrange("(kc p) n -> p kc n", p=128)[:, :, c0:c1]

    def ada_slab(l, s):
        sl, v = load_slab(kview(w_ada_d[l], 512 * s, 512 * s + 512), 8, 512)
        for cc in range(4):
            ch = 4 * s + cc
            for kc in range(8):
                mm(MODP_AP[:, 2 * ch:2 * ch + 2], v[:, kc, cc * 128:(cc + 1) * 128], SCB.t[:, kc, :],
                   kc == 0, kc == 7, [sl, SCB], [SSA])

    def ada_finish(l, c0=0, c1=48):
        mp = MODP_AP.rearrange("p (c v) -> p c v", v=2)
        for v_ in range(2):
            tt(MOD[l % 2].t[:, c0:c1, v_], mp[:, c0:c1, v_], BADA.t[:, l, c0:c1], ALU.add, [SSA, BADA], [MOD[l % 2]])

    def ada_lane(items, start, spacing, lead):
        slabs = []
        fins = {}
        for it in items:
            if it[0] == "slab":
                slabs.append(it)
            else:
                fins.setdefault(len(slabs) - 1, []).append(it)
        n = len(slabs)
        loaded = {}
        r = 0
        k_dma = 0
        k_use = 0
        for f in fins.get(-1, []):
            ada_finish(f[1], f[2], f[3])
        while k_use < n:
            if k_dma < n and k_dma < k_use + NSLAB and r >= start + spacing * k_dma:
                _, l_, s_ = slabs[k_dma]
                loaded[k_dma] = load_slab(kview(w_ada_d[l_], 512 * s_, 512 * s_ + 512), 8, 512)
                k_dma += 1
            if k_use < k_dma and r >= start + spacing * k_use + lead:
                _, l_, s_ = slabs[k_use]
                sl, v = loaded.pop(k_use)
                for cc in range(4):
                    ch = 4 * s_ + cc
                    for kc in range(8):
                        mm(MODP_AP[:, 2 * ch:2 * ch + 2], v[:, kc, cc * 128:(cc + 1) * 128], SCB.t[:, kc, :],
                           kc == 0, kc == 7, [sl, SCB], [SSA])
                for f in fins.get(k_use, []):
                    ada_finish(f[1], f[2], f[3])
                k_use += 1
            yield
            r += 1

    def norm(l, which):
        mod = MOD[l % 2]
        A = A1 if which == 1 else A2
        boff = 0 if which == 1 else 24
        sums = []
        for u in range(3):
            t0 = 256 * u
            ss = SSA if u == 0 else MM.next()
            for k in range(8):
                sq = SQ.next()
                if k % 2 == 0:
                    act(sq.t, X[k].t[:, t0:t0 + 256], AF.Square, [X[k]], [sq])
                else:
                    tt(sq.t, X[k].t[:, t0:t0 + 256], X[k].t[:, t0:t0 + 256], ALU.mult, [X[k]], [sq])
                mm(ss.t[:, 0:256], ONEB.t, sq.t, k == 0, k == 7, [sq, ONEB], [ss])
            sums.append(ss)
        rstds = []
        for u in range(3):
            std = STD.next() if u < 2 else TMPF.next()
            rsqrt_act(std, std.t, sums[u].t[:, 0:256], [sums[u]], 1.0 / 1024)
            rstds.append(std)
        for u in range(3):
            t0 = 256 * u
            v_ = 1 if u == 2 else 0
            rstd = rstds[u]
            for k in range(8):
                tmp = TMPF.next()
                if tmp is rstds[2]:
                    tmp = TMPF.next()
                stt(tmp.t, X[k].t[:, t0:t0 + 256], A.t[:, k, v_:v_ + 1], rstd.t, ALU.mult, ALU.mult,
                    [X[k], A, rstd], [tmp])
                if k % 4 == 3:
                    ts(H[k].t[:, t0:t0 + 256], tmp.t, mod.t[:, boff + k, v_:v_ + 1], ALU.add, [tmp, mod], [H[k]])
                else:
                    act(H[k].t[:, t0:t0 + 256], tmp.t, AF.Identity, [tmp, mod], [H[k]],
                        bias=mod.t[:, boff + k, v_:v_ + 1])

    def gates(l, u):
        t0 = 256 * u
        for ch in range(4):
            ps = MM.next()
            mm(ps.t[:, 0:256], WG.t[0:32, ch * 128:(ch + 1) * 128], ZF.t[0:32, t0:t0 + 256], True, True,
               [WG, ZF], [ps])
            e = TMPF.next()
            nb = NEGB.t[:, (0 if ch < 2 else 8) + 2 * l + (ch % 2):(0 if ch < 2 else 8) + 2 * l + (ch % 2) + 1]
            act(e.t, ps.t[:, 0:256], AF.Exp, [ps, NEGB], [e], bias=nb, scale=-1.0)
            act(SP[ch].t, e.t, AF.Ln, [e, COLS], [SP[ch]], bias=ONE)
            P.add("dve", lambda eng, o=CS[ch].t, d1=SP[ch].t: eng.tensor_tensor_scan(
                out=o, data0=ONEF.t, data1=d1, initial=0.0, op0=ALU.mult, op1=ALU.add),
                reads=[ONEF, SP[ch]], writes=[CS[ch]])
            if ch >= 2:
                tt(SP[ch].t, CS[ch].t, SP[ch].t, ALU.subtract, [CS[ch], SP[ch]], [SP[ch]])

    def gla_prep(u, pr, S, K):
        t0 = 256 * u
        for d in range(2):
            bias_, eqs, eks, k32s, psts = [], [], [], [], []
            for c in range(2):
                bia = BIA.hold()
                bias_.append(bia)
                if d == 0:
                    if c == 0:
                        memset(bia.t, 0.0, [bia])
                    else:
                        ts(bia.t[:, 0:1], CS[pr].t[:, 127:128], 1.0 / 16, ALU.mult, [CS[pr]], [bia])
                        ts(bia.t[:, 1:2], CS[pr].t[:, 127:128], -1.0 / 16, ALU.mult, [CS[pr]], [bia])
                else:
                    col = 128 * c + 127
                    ts(bia.t[:, 0:1], CS[2 + pr].t[:, col:col + 1], -1.0 / 16, ALU.mult, [CS[2 + pr]], [bia])
                    ts(bia.t[:, 1:2], CS[2 + pr].t[:, col:col + 1], 1.0 / 16, ALU.mult, [CS[2 + pr]], [bia])
            yield
            src_t = CS[pr] if d == 0 else SP[2 + pr]
            sq_, sk_ = (-1.0 / 16, 1.0 / 16) if d == 0 else (1.0 / 16, -1.0 / 16)
            for c in range(2):
                loc = slice(128 * c, 128 * c + 128)
                bia = bias_[c]
                eq = EX.hold()
                act(eq.t, src_t.t[:, loc], AF.Exp, [src_t, bia], [eq], bias=bia.t[:, 0:1], scale=sq_)
                ek = EX.hold()
                act(ek.t, src_t.t[:, loc], AF.Exp, [src_t, bia], [ek], bias=bia.t[:, 1:2], scale=sk_)
                BIA.release(bia)
                eqs.append(eq)
                eks.append(ek)
            yield
            for c in range(2):
                tok = slice(t0 + 128 * c, t0 + 128 * c + 128)
                eq, ek = eqs[c], eks[c]
                stt(S.qt[d][c].t, GQ[pr].t[:, tok], 0.125, eq.t, ALU.mult, ALU.mult, [GQ[pr], eq], [S.qt[d][c]])
                k32 = KT32.hold()
                k32s.append(k32)
                tt(k32.t, GK[pr].t[:, tok], ek.t, ALU.mult, [GK[pr], ek], [k32])
                ecol = eq.t[:, 127:128] if d == 0 else eq.t[:, 0:1]
                cp(S.e[d][c].t, ecol, [eq], [S.e[d][c]], eng="dve")
                EX.release(eq)
                EX.release(ek)
            yield
            pst = ST.hold()
            for c in range(2):
                k32 = k32s[c]
                cp(S.kt[d][c].t, k32.t, [k32], [S.kt[d][c]], eng="act")
                P.add("pe", lambda e, o=pst.t[:, 128 * c:128 * c + 128], i=k32.t: e.transpose(o, i, IDF),
                      reads=[k32, CST], writes=[pst])
                KT32.release(k32)
            yield
            for c in range(2):
                o_ = 128 * c
                cp(K.km0[d][c].t[:, 0:64], pst.t[:, o_:o_ + 64], [pst], [K.km0[d][c]], eng="act")
                cp(K.km1[d][c].t[:, 64:128], pst.t[:, o_ + 64:o_ + 128], [pst], [K.km1[d][c]], eng="dve")
            ST.release(pst)
            yield

    def gla_pass1(u, pr, S, K):
        def kv(pk, d, c):
            g = GV[2 * u + c]
            o_ = 128 * d
            mm(pk.t[:, o_:o_ + 128], K.km0[d][c].t, g.t[:, (2 * pr) * 128:(2 * pr) * 128 + 128], True, False,
               [K.km0[d][c], g], [pk])
            mm(pk.t[:, o_:o_ + 128], K.km1[d][c].t, g.t[:, (2 * pr + 1) * 128:(2 * pr + 1) * 128 + 128], False, True,
               [K.km1[d][c], g], [pk])
        firsts = (0, 1)
        seconds = (1, 0)
        pk = ST.hold()
        for d in range(2):
            kv(pk, d, firsts[d])
        yield
        for d in range(2):
            ts(S.mid[d].t, pk.t[:, 128 * d:128 * d + 128], S.e[d][firsts[d]].t, ALU.mult,
               [pk, S.e[d][firsts[d]]], [S.mid[d]])
        ST.release(pk)
        pk = ST.hold()
        for d in range(2):
            kv(pk, d, seconds[d])
        yield
        for d in range(2):
            tmp = TMPS.next()
            tt(tmp.t, pk.t[:, 128 * d:128 * d + 128], S.mid[d].t, ALU.add, [pk, S.mid[d]], [tmp])
            ts(S.fin[d].t, tmp.t, S.e[d][seconds[d]].t, ALU.mult, [tmp, S.e[d][seconds[d]]], [S.fin[d]])
        ST.release(pk)
        yield

    def gla_pass2(l, u, pr, S, have_init):
        t0 = 256 * u
        po = OD.hold()
        for c in range(2):
            g = GV[2 * u + c]
            pas = [ST.hold(), ST.hold()]
            for hh in range(2):
                prow = slice(hh * 64, hh * 64 + 64)
                for d in range(2):
                    o_ = 128 * d
                    mm(pas[hh].t[:, o_:o_ + 128], S.kt[d][c].t[prow, :], S.qt[d][c].t[prow, :], True, True,
                       [S.kt[d][c], S.qt[d][c]], [pas[hh]])
            yield
            am = AM.hold()
            for hh in range(2):
                tt(am.t[:, 256 * hh:256 * hh + 256], pas[hh].t[:, 0:256], TRI4.t[:, 0:256], ALU.mult,
                   [pas[hh], TRI4], [am])
                ST.release(pas[hh])
            for hh in range(2):
                h = 2 * pr + hh
                prow = slice(hh * 64, hh * 64 + 64)
                seq = []
                for d in range(2):
                    o_ = 256 * hh + 128 * d
                    seq.append((g.t[:, h * 128:h * 128 + 128], am.t[:, o_:o_ + 128], [g, am]))
                    first_chunk = (c == 0) if d == 0 else (c == 1)
                    if first_chunk:
                        if have_init:
                            seq.append((S.inib[d].t[prow, :], S.qt[d][c].t[prow, :], [S.inib[d], S.qt[d][c]]))
                    else:
                        seq.append((S.midb[d].t[prow, :], S.qt[d][c].t[prow, :], [S.midb[d], S.qt[d][c]]))
                for i, (lh, rh, rd_) in enumerate(seq):
                    o_ = 256 * hh + 128 * c
                    mm(po.t[:, o_:o_ + 128], lh, rh, i == 0, i == len(seq) - 1, rd_, [po])
            AM.release(am)
            yield
        sqo = SQB.hold()
        act(sqo.t, po.t, AF.Square, [po], [sqo])
        yield
        pss = ST.hold()
        mm(pss.t, ONEB.t, sqo.t, True, True, [sqo, ONEB], [pss])
        SQB.release(sqo)
        yield
        std = STD5.hold()
        rsqrt_act(std, std.t, pss.t, [pss], 1.0 / 128)
        ST.release(pss)
        yield
        tmp = RAW.hold()
        stt(tmp.t, po.t, GVEC.t[:, G_OUT + l:G_OUT + l + 1], std.t, ALU.mult, ALU.mult, [po, GVEC, std], [tmp])
        for hh in range(2):
            h = 2 * pr + hh
            tt(H[4 + h].t[:, t0:t0 + 256], tmp.t[:, 256 * hh:256 * hh + 256], SG[h].t[:, t0:t0 + 256], ALU.mult,
               [tmp, SG[h]], [H[4 + h]])
        RAW.release(tmp)
        STD5.release(std)
        OD.release(po)
        yield

    def gla_prompt_job(l, u, pr):
        S = GS_P.next()
        K = KMS.hold()
        yield from gla_prep(u, pr, S, K)
        yield from gla_pass1(u, pr, S, K)
        KMS.release(K)
        for d in range(2):
            cp(S.midb[d].t, S.mid[d].t, [S.mid[d]], [S.midb[d]], eng="act")
            P.dma("sp", (nsf_d if d == 0 else nsb_d)[l, u][:, pr, :], S.fin[d].t, reads=[S.fin[d]], store=True)
        yield
        yield from gla_pass2(l, u, pr, S, False)

    def na_head(l, u, pr, hh, chunks):
        t0 = 256 * u
        prow = slice(hh * 64, hh * 64 + 64)
        pod = OD.hold()
        qp = QP[hh].hold()
        cp(qp.t[prow, :], QN[pr].t[prow, t0:t0 + 256], [QN[pr]], [qp], eng="act" if hh == 0 else "dve")
        n = len(chunks)
        masked = [i0 for i0 in range(0, n, 2) if chunks[i0][4] is not None]
        mks = {}

        def prefetch(cnt):
            for i0 in masked:
                if cnt <= 0:
                    break
                if i0 not in mks:
                    ml, mh, mm_ = chunks[i0][4]
                    mk_ = MK.hold()
                    P.dma("sp" if pr % 2 == 0 else "act", mk_.t, mask_d[ml, mh][:, 256 * mm_:256 * mm_ + 512],
                          writes=[mk_])
                    mks[i0] = mk_
                    cnt -= 1

        prefetch(3)
        for i0 in range(0, n, 2):
            pair = chunks[i0:i0 + 2]
            ps = ST.hold()
            for ii, (kap, kreads, vap, vreads, msk, mreads) in enumerate(pair):
                mm(ps.t[:, 256 * ii:256 * ii + 256], kap, qp.t, True, True, kreads + [qp], [ps])
            mk = mks.get(i0)
            yield
            pt = PT.hold()
            if mk is None:
                act(pt.t, ps.t, AF.Exp, [ps], [pt], scale=0.125)
                ST.release(ps)
            else:
                stt(ps.t, ps.t, 0.125, mk.t, ALU.mult, ALU.add, [ps, mk], [ps])
                MK.release(mk)
                masked.remove(i0)
                del mks[i0]
                prefetch(3 - len(mks))
                yield
                act(pt.t, ps.t, AF.Exp, [ps], [pt])
                ST.release(ps)
            yield
            for ii, (kap, kreads, vap, vreads, msk, mreads) in enumerate(pair):
                i = i0 + ii
                mm(pod.t[:, 0:256], vap, pt.t[:, 256 * ii:256 * ii + 256], i == 0, i == n - 1, vreads + [pt], [pod], skip=True)
                mm(pod.t[:, 256:512], ONEB.t, pt.t[:, 256 * ii:256 * ii + 256], False, i == n - 1, [ONEB, pt], [pod], skip=True)
            PT.release(pt)
        yield
        rd = RD.hold()
        act(rd.t[prow, :], pod.t[prow, 256:512], AF.Ln, [pod], [rd])
        act(rd.t[prow, :], rd.t[prow, :], AF.Exp, [rd], [rd], scale=-1.0)
        yield
        tt(H[pr].t[prow, t0:t0 + 256], pod.t[prow, 0:256], rd.t[prow, :], ALU.mult, [pod, rd], [H[pr]])
        RD.release(rd)
        OD.release(pod)
        QP[hh].release(qp)
        yield

    def layer(l):
        mod = MOD[l % 2]
        for v_ in range(2):
            stt(A1.t[:, :, v_], mod.t[:, 8:16, v_], 1.0, GVEC.t[:, G_ATT + 8 * l:G_ATT + 8 * l + 8],
                ALU.add, ALU.mult, [mod, GVEC], [A1])
        norm(l, 1)
        P.dma("pool", CKB.t[:, :, 0:512], ckT_d[l][:, :, 0:512], writes=[CKB])
        P.dma("pool", CVB.t[:, :, 0:512], cv_d[l][:, :, 0:512], writes=[CVB])
        P.dma("sp", WG.t, wg_d[:, l, :], writes=[WG])
        P.dma("sp", STIN[0].t[:], stf_d[l], writes=[STIN[0]])
        P.dma("sp", STIN[1].t[:], stb_d[l], writes=[STIN[1]])
        P.add("dve", lambda e: e.memset(DUM.t[:, 0:1], 0.0), writes=[DUM] + H1 + att_tiles)

        qk_tail = []
        for s in range(6):
            sl, v = load_slab(kview(w_in_d[l], 512 * s, 512 * s + 512), 8, 512)
            if s == 2 and qk_tail:
                qk_tail.pop()()
            if s in (2, 4):
                for blk in range(6):
                    ps = MM.next()
                    for kc in range(8):
                        mm(ps.t, H[kc].t[:, blk * 128:blk * 128 + 128], v[:, kc, 0:512], kc == 0, kc == 7,
                           [sl, H[kc]], [ps])
                    if s == 2:
                        if blk < 4:
                            vst = VST.next()
                            cp(vst.t, ps.t, [ps], [vst], eng="act")
                            P.dma("sp", nv_d[l][:, blk, :], vst.t, reads=[vst], store=True)
                            cp(VOWN[blk].t, ps.t, [ps], [VOWN[blk]], eng="dve")
                        else:
                            cp(VS.t[:, blk - 4, :], ps.t, [ps], [VS], eng="act" if blk == 4 else "dve")
                    else:
                        cp(GV[blk].t, ps.t, [ps], [GV[blk]], eng="act" if blk % 2 == 0 else "dve")
                continue
            for cc in range(4):
                for (t0, n, v_) in tiles_tok:
                    ps = MM.next()
                    for kc in range(8):
                        mm(ps.t[:, 0:n], v[:, kc, cc * 128:cc * 128 + 128], H[kc].t[:, t0:t0 + n], kc == 0, kc == 7,
                           [sl, H[kc]], [ps])
                    if s in (0, 1):
                        sqb = SQB.hold()
                        act(sqb.t[:, 0:n], ps.t[:, 0:n], AF.Square, [ps], [sqb])
                        raw = RAW.hold()
                        cp(raw.t[:, 0:n], ps.t[:, 0:n], [ps], [raw], eng="dve")

                        def tail(s=s, cc=cc, t0=t0, n=n, sqb=sqb, raw=raw):
                            ps2 = MM.next()
                            mm(ps2.t[:, 0:n], BLKB.t, sqb.t[:, 0:n], True, True, [BLKB, sqb], [ps2])
                            SQB.release(sqb)
                            std = STD5.next()
                            rsqrt_act(std, std.t[:, 0:n], ps2.t[:, 0:n], [ps2], 1.0 / 64)
                            rstd = std
                            gcol = GVEC.t[:, (G_Q if s == 0 else G_K) + l:(G_Q if s == 0 else G_K) + l + 1]
                            dst = QN[cc] if s == 0 else KN[cc]
                            stt(dst.t[:, t0:t0 + n], raw.t[:, 0:n], gcol, rstd.t[:, 0:n], ALU.mult, ALU.mult,
                                [raw, GVEC, rstd], [dst])
                            if s == 1 and t0 == 0:
                                knf = KNF.next()
                                stt(knf.t, raw.t[:, 0:512], gcol, rstd.t[:, 0:512], ALU.mult, ALU.mult,
                                    [raw, GVEC, rstd], [knf])
                                P.dma("sp", nkT_d[l][:, cc, :], knf.t, reads=[knf], store=True)
                            RAW.release(raw)
                        if qk_tail:
                            qk_tail.pop()()
                        qk_tail.append(tail)
                    elif s == 3:
                        dst = GQ[cc] if cc < 2 else GK[cc - 2]
                        if v_ == 0:
                            cp(dst.t[:, t0:t0 + n], ps.t[:, 0:n], [ps], [dst], eng="act")
                        else:
                            raw = RAW.next()
                            cp(raw.t[:, 0:256], ps.t[:, 0:256], [ps], [raw], eng="act")
                            ps2 = MM.next()
                            mm(ps2.t[:, 0:256], ROTF, raw.t[:, 0:256], True, True, [CST, raw], [ps2])
                            t1 = TMPF.next()
                            tt(t1.t, raw.t[:, 0:256], COS, ALU.mult, [raw, CST], [t1])
                            t2 = TMPF.next()
                            tt(t2.t, ps2.t[:, 0:256], SIN, ALU.mult, [ps2, CST], [t2])
                            tt(dst.t[:, 512:768], t1.t, t2.t, ALU.add, [t1, t2], [dst])
                    else:
                        act(SG[cc].t[:, t0:t0 + n], ps.t[:, 0:n], AF.Silu, [ps], [SG[cc]])
        P.dma("pool", WZ.t[:, :, 0:32], kview(w_in_d[l], 3072, 3104), writes=[WZ])
        for (t0, n, v_) in tiles_tok:
            ps = MM.next()
            for kc in range(8):
                mm(ps.t[0:32, 0:n], WZ.t[:, kc, 0:32], H[kc].t[:, t0:t0 + n], kc == 0, kc == 7, [WZ, H[kc]], [ps])
            cp(ZF.t[0:32, t0:t0 + n], ps.t[0:32, 0:n], [ps], [ZF], eng="act")

        gates(l, 2)

        def sample_pre(pr):
            S = GS_S[pr]
            K = KMS.hold()
            yield from gla_prep(2, pr, S, K)
            yield from gla_pass1(2, pr, S, K)
            KMS.release(K)
            cp(STG2.t[:, pr * 256:pr * 256 + 128], S.fin[0].t, [S.fin[0]], [STG2], eng="act")
            cp(STG2.t[:, pr * 256 + 128:pr * 256 + 256], S.fin[1].t, [S.fin[1]], [STG2], eng="act")
            for d in range(2):
                tt(STG2.t[:, 512 + 2 * pr + d:512 + 2 * pr + d + 1], S.e[d][0].t, S.e[d][1].t, ALU.mult,
                   [S.e[d][0], S.e[d][1]], [STG2])
            yield

        c1v = cc1_in.ap().rearrange("p (a b) -> p a b", b=512)
        P.dma("sp", c1v[:, :, 0:256], kn_t[:, :, 512:768], reads=KN, writes=[CC1I], key="cc1w")
        for mo in range(2):
            P.dma("sp", c1v[:, :, 256 + 128 * mo:256 + 128 * mo + 128],
                  VS.t[:, mo, :].rearrange("p (a b) -> p a b", b=128), reads=[VS], writes=[CC1I], key="cc1w")
        P.add("pool", lambda e: e.collective_compute(
            "AllGather", ALU.bypass, replica_groups=[[0, 1, 2, 3], [4, 5, 6, 7]],
            ins=[cc1_in.ap().opt()], outs=[cc1_out.ap().opt()]),
            reads=[CC1I], writes=[CC1O], key="cc", inc=1)
        memset(STG2.t[:, 516:520], 0.0, [STG2])

        OH = CST.t[:, C_OH:C_OH + 4]

        g2_state = {"loaded": False}

        def ensure_g2():
            if not g2_state["loaded"]:
                P.dma("sp", G2.t[:], cc2_out.ap().rearrange("(j p) x -> p j x", p=128), reads=[CC2O], writes=[G2])
                g2_state["loaded"] = True

        def fold_job(pr):
            ensure_g2()
            S = GS_S[pr]
            W = F512.hold()
            for d in range(2):
                ini = W.t[:, 128 * d:128 * d + 128]
                blk = [W.t[:, 256:384], W.t[:, 384:512]]
                cur = blk[0]
                cp(cur, STIN[d].t[:, pr, :], [STIN[d]], [W], eng="act")
                order = [0, 1, 2, 3] if d == 0 else [3, 2, 1, 0]
                ts(ini, cur, OH[:, order[0]:order[0] + 1], ALU.mult, [W, CST], [W])
                for idx in range(3):
                    j = order[idx]
                    nxt = blk[(idx + 1) % 2]
                    dcol = G2.t[:, j, 512 + 2 * pr + d:512 + 2 * pr + d + 1]
                    acol = G2.t[:, j, pr * 256 + 128 * d:pr * 256 + 128 * d + 128]
                    stt(nxt, cur, dcol, acol, ALU.mult, ALU.add, [W, G2], [W])
                    jn = order[idx + 1]
                    stt(ini, nxt, OH[:, jn:jn + 1], ini, ALU.mult, ALU.add, [W, CST], [W])
                    cur = nxt
                    yield
                cp(S.inib[d].t, ini, [W], [S.inib[d]], eng="act")
                c_first = 0 if d == 0 else 1
                stt(S.midb[d].t, ini, S.e[d][c_first].t, S.mid[d].t, ALU.mult, ALU.add,
                    [W, S.e[d][c_first], S.mid[d]], [S.midb[d]])
                yield
            F512.release(W)
            yield from gla_pass2(l, 2, pr, S, True)

        def gla_lane():
            for pr in range(2):
                yield from sample_pre(pr)
            P.dma("sp", cc2_in.ap(), STG2.t, reads=[STG2], writes=[CC2I], key="cc2w")
            P.add("pool", lambda e: e.collective_compute(
                "AllGather", ALU.bypass, replica_groups=[[0, 1, 2, 3], [4, 5, 6, 7]],
                ins=[cc2_in.ap().opt()], outs=[cc2_out.ap().opt()]),
                reads=[CC2I], writes=[CC2O], key="cc", inc=1)
            yield
            for u in range(2):
                gates(l, u)
                yield
                for pr in range(2):
                    yield from gla_prompt_job(l, u, pr)
            yield from fold_job(0)

        g1v = cc1_out.ap().rearrange("(j p) x -> p j x", p=128)
        gp_of = {}

        def sample_head(pr, hh):
            if hh == 0:
                gp = GP.hold()
                gp_of[pr] = gp
                P.dma("pool", gp.t[:], g1v[:, :, pr * 512:pr * 512 + 512], reads=[CC1O], writes=[gp])
            gp = gp_of[pr]
            h = 2 * pr + hh
            prow = slice(hh * 64, hh * 64 + 64)
            chunks = []
            for kc in range(4):
                chunks.append((CKB.t[:, pr, 128 * kc:128 * kc + 128], [CKB],
                               CVB.t[:, kc, 128 * pr:128 * pr + 128], [CVB], None, []))
            for m in range(8):
                j, mo = m // 2, m % 2
                chunks.append((gp.t[:, j, 128 * mo:128 * mo + 128], [gp],
                               gp.t[:, j, 256 + 128 * mo:256 + 128 * mo + 128], [gp],
                               (l, h, m), None))
            yield from na_head(l, 2, pr, hh, chunks)
            if hh == 1:
                GP.release(gp)

        items = []
        if l == 0:
            items += [("slab", 0, s_) for s_ in range(4, 12)] + [("fin", 0, 16, 48)]
        if l + 1 < n_layers:
            items += [("slab", l + 1, s_) for s_ in range(12)] + [("fin", l + 1, 0, 48)]
        lane_done = {}

        def na_lane(prs):
            for u in range(2):
                t0 = 256 * u
                for pr in prs:
                    for hh in range(2):
                        prow = slice(hh * 64, hh * 64 + 64)
                        chunks = []
                        for m in range(2):
                            chunks.append((KN[pr].t[:, t0 + 128 * m:t0 + 128 * m + 128], [KN[pr]],
                                           VOWN[2 * u + m].t[:, pr * 128:pr * 128 + 128], [VOWN[2 * u + m]], None, []))
                        yield from na_head(l, u, pr, hh, chunks)
            for pr in prs:
                for hh in range(2):
                    yield from sample_head(pr, hh)
            lane_done[prs] = True
            if prs == (1, 3):
                while not lane_done.get((0, 2)):
                    yield
                yield from fold_job(1)

        run_pool([na_lane((0, 2)), na_lane((1, 3)), gla_lane(), ada_lane(items, 16, 9 if l > 0 else 5, 24 if l > 0 else 12)], 4)

        for s in range(2):
            sl, v = load_slab(kview(w_o_d[l], 512 * s, 512 * s + 512), 8, 512)
            for cc in range(4):
                ch = 4 * s + cc
                for (t0, n, v_) in tiles_tok:
                    ps = MM.next()
                    for kc in range(8):
                        mm(ps.t[:, 0:n], v[:, kc, cc * 128:cc * 128 + 128], H[kc].t[:, t0:t0 + n], kc == 0, kc == 7,
                           [sl, H[kc]], [ps])
                    stt(X[ch].t[:, t0:t0 + n], ps.t[:, 0:n], mod.t[:, 16 + ch, v_:v_ + 1], X[ch].t[:, t0:t0 + n],
                        ALU.mult, ALU.add, [ps, mod, X[ch]], [X[ch]])

        for v_ in range(2):
            stt(A2.t[:, :, v_], mod.t[:, 32:40, v_], 1.0, GVEC.t[:, G_MLP + 8 * l:G_MLP + 8 * l + 8],
                ALU.add, ALU.mult, [mod, GVEC], [A2])
        norm(l, 2)
        P.add("dve", lambda e: e.memset(DUM.t[:, 1:2], 0.0), writes=[DUM] + H1 + att_tiles)
        for s in range(8):
            sl, v = load_slab(kview(w_up_d[l], 512 * s, 512 * s + 512), 8, 512)
            for cc in range(4):
                for (t0, n, v_) in tiles_tok:
                    ps = MM.next()
                    for kc in range(8):
                        mm(ps.t[:, 0:n], v[:, kc, cc * 128:cc * 128 + 128], H[kc].t[:, t0:t0 + n], kc == 0, kc == 7,
                           [sl, H[kc]], [ps])
                    r = RL.next()
                    act(r.t[:, 0:n], ps.t[:, 0:n], AF.Relu, [ps], [r])
                    tt(H1[s].t[:, cc, t0:t0 + n], r.t[:, 0:n], r.t[:, 0:n], ALU.mult, [r], [H1[s]])
        for mc in range(8):
            src = w_down_d[l].rearrange("(kc p) n -> p kc n", p=128)[:, :, mc * 128:mc * 128 + 128]
            sl, v = load_slab(src, 32, 128)
            for (t0, n, v_) in tiles_tok:
                ps = MM.next()
                for kc in range(32):
                    mm(ps.t[:, 0:n], v[:, kc, 0:128], H1[kc // 4].t[:, kc % 4, t0:t0 + n], kc == 0, kc == 31,
                       [sl, H1[kc // 4]], [ps])
                stt(X[mc].t[:, t0:t0 + n], ps.t[:, 0:n], mod.t[:, 40 + mc, v_:v_ + 1], X[mc].t[:, t0:t0 + n],
                    ALU.mult, ALU.add, [ps, mod, X[mc]], [X[mc]])

    for s in range(4):
        ada_slab(0, s)
    ada_finish(0, 0, 16)
    for l in range(n_layers):
        layer(l)
    P.dma("sp", yT_d, x_t[:], reads=X, store=True, key="ystore")
    P.emit(nc)
    return nc, P.stats + " sbuf_peak=%d" % ar.peak


def _consts(j):
    cst = np.zeros((128, NCST), np.float32)
    cst[:, C_ID:C_ID + 128] = np.eye(128, dtype=np.float32)
    blk = np.zeros((128, 128), np.float32)
    blk[:64, :64] = 1.0
    blk[64:, 64:] = 1.0
    cst[:, C_BLK:C_BLK + 128] = blk
    rot = np.zeros((128, 128), np.float32)
    for m in range(128):
        part = (m % 32) // 16
        if part == 0:
            rot[m + 16, m] = -1.0
        else:
            rot[m - 16, m] = 1.0
    cst[:, C_ROT:C_ROT + 128] = rot
    jj = np.arange(128)[:, None]
    ii = np.arange(128)[None, :]
    cst[:, C_TRIU:C_TRIU + 128] = (ii >= jj).astype(np.float32)
    cst[:, C_TRIL:C_TRIL + 128] = (ii <= jj).astype(np.float32)
    inv = (np.float32(10000.0) ** (-np.arange(16, dtype=np.float32) / np.float32(16))).astype(np.float32)
    t = 256 * j + np.arange(256)
    row = (t // 64).astype(np.float32)
    col = (t % 64).astype(np.float32)
    for p in range(128):
        dim = p % 64
        half = dim // 32
        i = (dim % 32) % 16
        ang = ((row if half == 0 else col) * inv[i]).astype(np.float32)
        cst[p, C_COS:C_COS + 256] = np.cos(ang).astype(np.float32)
        cst[p, C_SIN:C_SIN + 256] = np.sin(ang).astype(np.float32)
    cst[:, C_OH + j] = 1.0
    return cst


def _mask(rpb, j):
    out = np.full((4, 8, 128, 8, 256), NEG, np.float32)
    qc = np.arange(64)
    cs = np.clip(qc - 8, 0, 48)
    kc = np.arange(64)
    valid = (kc[:, None] >= cs[None, :]) & (kc[:, None] <= cs[None, :] + 15)
    dc = np.clip(kc[:, None] - qc[None, :] + 15, 0, 30)
    for rl in range(4):
        r = 4 * j + rl
        rs = min(max(r - 4, 0), 8)
        for m in range(8):
            for par in range(2):
                kr = 2 * m + par
                if not (rs <= kr <= rs + 7):
                    continue
                dr = kr - r + 7
                vals = rpb[:, :, dr][:, :, dc]
                blk = np.where(valid[None, None], vals, np.float32(NEG))
                out[:, :, par * 64:(par + 1) * 64, m, rl * 64:(rl + 1) * 64] = blk
    return np.ascontiguousarray(out.reshape(4, 8, 128, 2048))


_CACHE = {}


def kernel(x_prompt, x_sample, cache_k, cache_v, state_fwd, state_bwd, c, c_ctx,
           w_ada, b_ada, g_attn, w_in, g_q, g_k, rpb, w_gf, b_gf, w_gb, b_gb,
           g_gla_out, w_o, g_mlp, w_up, w_down):
    f = lambda a: np.ascontiguousarray(np.asarray(a, dtype=np.float32))
    x_prompt, x_sample, cache_k, cache_v = f(x_prompt), f(x_sample), f(cache_k), f(cache_v)
    state_fwd, state_bwd, c, c_ctx = f(state_fwd), f(state_bwd), f(c), f(c_ctx)
    w_ada, w_in, w_o, w_up, w_down = f(w_ada), f(w_in), f(w_o), f(w_up), f(w_down)
    b_ada, g_attn, g_q, g_k, rpb = f(b_ada), f(g_attn), f(g_q), f(g_k), f(rpb)
    w_gf, b_gf, w_gb, b_gb, g_gla_out, g_mlp = f(w_gf), f(b_gf), f(w_gb), f(b_gb), f(g_gla_out), f(g_mlp)

    badaT = np.ascontiguousarray(b_ada.reshape(4, 48, 128).transpose(2, 0, 1))
    gvec = np.zeros((128, NGV), np.float32)
    gvec[:, G_ATT:G_ATT + 32] = g_attn.reshape(4, 8, 128).transpose(2, 0, 1).reshape(128, 32)
    gvec[:, G_MLP:G_MLP + 32] = g_mlp.reshape(4, 8, 128).transpose(2, 0, 1).reshape(128, 32)
    gvec[:, G_Q:G_Q + 4] = np.tile(g_q.T, (2, 1))
    gvec[:, G_K:G_K + 4] = np.tile(g_k.T, (2, 1))
    gvec[:, G_OUT:G_OUT + 4] = g_gla_out.T
    gvec[:, G_BF:G_BF + 8] = b_gf.reshape(4, 2, 128).transpose(2, 0, 1).reshape(128, 8)
    gvec[:, G_BB:G_BB + 8] = b_gb.reshape(4, 2, 128).transpose(2, 0, 1).reshape(128, 8)
    wg = np.zeros((32, 4, 512), np.float32)
    wg[0:16, :, 0:256] = w_gf.transpose(1, 0, 2)
    wg[16:32, :, 256:512] = w_gb.transpose(1, 0, 2)

    per_b = []
    for b in range(2):
        ckT = np.zeros((4, 128, 4, 520), np.float32)
        ckT[:, :, :, 0:512] = cache_k[b].reshape(4, 4, 2, 512, 64).transpose(0, 2, 4, 1, 3).reshape(4, 128, 4, 512)
        cv = np.zeros((4, 128, 4, 520), np.float32)
        cv[:, :, :, 0:512] = cache_v[b].reshape(4, 8, 4, 128, 64).transpose(0, 3, 2, 1, 4).reshape(4, 128, 4, 512)
        stf = np.ascontiguousarray(state_fwd[b].reshape(4, 2, 2, 64, 128).transpose(0, 2, 3, 1, 4).reshape(4, 128, 2, 128))
        stb = np.ascontiguousarray(state_bwd[b].reshape(4, 2, 2, 64, 128).transpose(0, 2, 3, 1, 4).reshape(4, 128, 2, 128))
        per_b.append((ckT, cv, stf, stb))
    masks = [_mask(rpb, j) for j in range(4)]
    csts = [_consts(j) for j in range(4)]

    in_maps = []
    for core in range(8):
        b, j = core // 4, core % 4
        xs = np.concatenate([x_prompt[2 * core], x_prompt[2 * core + 1], x_sample[b, 256 * j:256 * j + 256]], axis=0)
        xT = np.ascontiguousarray(xs.reshape(768, 8, 128).transpose(2, 1, 0))
        cT = np.ascontiguousarray(np.stack([c_ctx, c[b]], axis=-1).reshape(8, 128, 2).transpose(1, 0, 2))
        ckT, cv, stf, stb = per_b[b]
        in_maps.append({
            "xT": xT, "cT": cT, "w_ada": w_ada, "w_in": w_in, "w_o": w_o, "w_up": w_up, "w_down": w_down,
            "badaT": badaT, "gvec": gvec, "wg": wg, "ckT": ckT, "cv": cv, "stf": stf, "stb": stb,
            "mask": masks[j], "cst": csts[j],
        })

    if "nc" not in _CACHE:
        _CACHE["nc"], _CACHE["stats"] = build_program(4)
    nc = _CACHE["nc"]
    res = run_bass_kernel_spmd(nc, in_maps, core_ids=list(range(8)))
    R = res.results

    y_prompt = np.zeros((16, 256, 1024), np.float32)
    y_sample = np.zeros((2, 1024, 1024), np.float32)
    new_k = np.zeros((16, 4, 8, 256, 64), np.float32)
    new_v = np.zeros((16, 4, 8, 256, 64), np.float32)
    new_sf = np.zeros((16, 4, 4, 64, 128), np.float32)
    new_sb = np.zeros((16, 4, 4, 64, 128), np.float32)
    for core in range(8):
        b, j = core // 4, core % 4
        r = R[core]
        y = np.asarray(r["yT"]).transpose(2, 1, 0).reshape(768, 1024)
        y_prompt[2 * core] = y[0:256]
        y_prompt[2 * core + 1] = y[256:512]
        y_sample[b, 256 * j:256 * j + 256] = y[512:768]
        nk = np.asarray(r["nkT"]).reshape(4, 2, 64, 4, 2, 256)
        new_k[2 * core:2 * core + 2] = nk.transpose(4, 0, 3, 1, 5, 2).reshape(2, 4, 8, 256, 64)
        nv = np.asarray(r["nv"]).reshape(4, 128, 2, 2, 8, 64)
        new_v[2 * core:2 * core + 2] = nv.transpose(2, 0, 4, 3, 1, 5).reshape(2, 4, 8, 256, 64)
        for name, dst in (("nsf", new_sf), ("nsb", new_sb)):
            s_ = np.asarray(r[name]).reshape(4, 2, 2, 64, 2, 128)
            dst[2 * core:2 * core + 2] = s_.transpose(1, 0, 4, 2, 3, 5).reshape(2, 4, 4, 64, 128)
    return (y_prompt, y_sample, new_k, new_v, new_sf, new_sb)
```

```python
import numpy as np
from contextlib import ExitStack
import concourse.bass as bass
import concourse.mybir as mybir
from concourse.bass_utils import run_bass_kernel_spmd

F32 = mybir.dt.float32
BF16 = mybir.dt.bfloat16
AF = mybir.ActivationFunctionType
ALU = mybir.AluOpType

SAME_ENGINE_SYNC = True
SAME_ENGINE_FULL = True
NSLAB = 3
NEG = -30000.0


class Tile:
    __slots__ = ("name", "t", "w", "r", "excl")

    def __init__(self, name, t=None, excl=False):
        self.name = name
        self.t = t
        self.w = {}
        self.r = []
        self.excl = excl

    def __getitem__(self, idx):
        return self.t[idx]


class Op:
    __slots__ = ("queue", "clock", "ordv", "fn", "deps", "waits", "has_waiter", "snap", "semval", "gi", "nraw")

    def __init__(self, queue, clock, fn):
        self.queue = queue
        self.clock = clock
        self.fn = fn
        self.deps = []
        self.waits = []
        self.has_waiter = False
        self.snap = None
        self.semval = None


class Prog:
    QUEUES = ("pe", "act", "dve", "pool", "sp")

    def __init__(self):
        self.ops = []
        self.clock_n = {}
        self.last_on_clock = {}
        self.stores = []

    def add(self, queue, fn, reads=(), writes=(), key=None, inc=16):
        clock = queue if key is None else ("dma", key, inc)
        op = Op(queue, clock, fn)
        n = self.clock_n.get(clock, 0) + 1
        self.clock_n[clock] = n
        op.ordv = n
        op.gi = len(self.ops)
        deps = []
        if key is not None:
            prev = self.last_on_clock.get(clock)
            if prev is not None:
                deps.append(prev)
        self.last_on_clock[clock] = op
        ex = [t for t in reads if t.excl]
        if ex:
            reads = [t for t in reads if not t.excl]
            writes = list(writes) + [t for t in ex if t not in writes]
        for t in reads:
            deps.extend(t.w.values())
        nraw = len(deps)
        for t in writes:
            deps.extend(t.w.values())
            deps.extend(t.r)
        op.nraw = nraw
        for t in reads:
            t.r.append(op)
        for t in writes:
            t.w[clock] = op
            t.r = []
        op.deps = deps
        self.ops.append(op)
        return op

    def dma(self, queue, out, in_, reads=(), writes=(), key=None, store=False):
        if key is None:
            key = (writes[0].name if writes else reads[0].name + "_st")

        def fn(eng, out=out, in_=in_):
            return eng.dma_start(out=out, in_=in_)
        op = self.add(queue, fn, reads=reads, writes=writes, key=key)
        if store:
            self.stores.append(op)
        return op

    def finalize(self):
        known = {q: {} for q in self.QUEUES}
        for op in self.ops:
            kq = known[op.queue]
            need = {}
            for di, d in enumerate(op.deps):
                if d.clock == op.clock:
                    if isinstance(op.clock, tuple):
                        pass
                    elif op.queue == "pe" or not SAME_ENGINE_SYNC:
                        continue
                    elif di >= op.nraw and not SAME_ENGINE_FULL:
                        continue
                cur = need.get(d.clock)
                if cur is None or cur.ordv < d.ordv:
                    need[d.clock] = d
            for d in sorted(need.values(), key=lambda d: -d.gi):
                if kq.get(d.clock, 0) >= d.ordv:
                    continue
                op.waits.append(d)
                d.has_waiter = True
                for c2, v2 in d.snap.items():
                    if kq.get(c2, 0) < v2:
                        kq[c2] = v2
                if kq.get(d.clock, 0) < d.ordv:
                    kq[d.clock] = d.ordv
            snap = dict(kq)
            if snap.get(op.clock, 0) < op.ordv:
                snap[op.clock] = op.ordv
            op.snap = snap
        cnt = {}
        for op in self.ops:
            if isinstance(op.clock, tuple):
                op.semval = op.clock[2] * op.ordv
            elif op.has_waiter:
                cnt[op.clock] = cnt.get(op.clock, 0) + 1
                op.semval = cnt[op.clock]
        for op in self.ops:
            op.snap = None

    def emit(self, nc, final_wait_queue="sp"):
        self.finalize()
        clocks = []
        seen = set()
        for op in self.ops:
            if op.clock not in seen and (isinstance(op.clock, tuple) or op.has_waiter):
                seen.add(op.clock)
                clocks.append(op.clock)
        with ExitStack() as es:
            sems = {}
            for i, c in enumerate(clocks):
                sems[c] = es.enter_context(nc.semaphore("s%d" % i))
            block = es.enter_context(nc.Block())
            byq = {q: [o for o in self.ops if o.queue == q] for q in self.QUEUES}
            stores = self.stores

            def run(eng, q):
                for op in byq[q]:
                    for d in op.waits:
                        eng.wait_ge(sems[d.clock], d.semval)
                    ins = op.fn(eng)
                    if isinstance(op.clock, tuple):
                        ins.then_inc(sems[op.clock], op.clock[2])
                    elif op.has_waiter:
                        ins.then_inc(sems[op.clock], 1)
                if q == final_wait_queue:
                    last = {}
                    for s in stores:
                        last[s.clock] = max(last.get(s.clock, 0), s.semval)
                    for c, v in last.items():
                        eng.wait_ge(sems[c], v)

            @block.tensor
            def _(e):
                run(e, "pe")

            @block.scalar
            def _(e):
                run(e, "act")

            @block.vector
            def _(e):
                run(e, "dve")

            @block.gpsimd
            def _(e):
                run(e, "pool")

            @block.sync
            def _(e):
                run(e, "sp")
        self.stats = "ops=%d %s waits=%d sems=%d" % (
            len(self.ops), {q: len(v) for q, v in byq.items()},
            sum(len(o.waits) for o in self.ops), len(clocks))


class Rot:
    def __init__(self, tiles):
        self.tiles = tiles
        self.i = 0
        self.held = set()

    def next(self):
        for _ in range(len(self.tiles)):
            t = self.tiles[self.i % len(self.tiles)]
            self.i += 1
            if t.name not in self.held:
                return t
        raise RuntimeError("pool exhausted: " + self.tiles[0].name)

    def hold(self):
        t = self.next()
        self.held.add(t.name)
        return t

    def release(self, t):
        self.held.discard(t.name)


def _dtsize(dt):
    return 2 if dt == BF16 else 4


class Arena:
    def __init__(self, nc):
        self.nc = nc
        self.off = 16512
        self.top = 229344
        self.peak = 0
        self.n = 0

    def alloc(self, name, shape, dt):
        nbytes = int(np.prod(shape[1:])) * _dtsize(dt)
        off = (self.off + 31) // 32 * 32
        if off + nbytes > self.top:
            raise RuntimeError("SBUF arena overflow at %s: need %d > %d" % (name, off + nbytes, self.top))
        self.n += 1
        t = self.nc.alloc_sbuf_tensor_at("%s_%d" % (name, self.n), list(shape), dt, offset=off)
        self.off = off + nbytes
        self.peak = max(self.peak, self.off)
        return t


C_ID, C_BLK, C_ROT, C_TRIU, C_TRIL, C_COS, C_SIN, C_OH = 0, 128, 256, 384, 512, 640, 896, 1152
NCST = 1156
G_ATT, G_MLP, G_Q, G_K, G_OUT, G_BF, G_BB = 0, 32, 64, 68, 72, 76, 84
NGV = 92


def build_program(n_layers=4):
    nc = bass.Bass("TRN2", target_bir_lowering=False)
    P = Prog()
    ar = Arena(nc)

    def din(name, shape, dt=F32):
        return nc.dram_tensor(name, list(shape), dt, kind="ExternalInput").ap()

    def dout(name, shape, dt=F32):
        return nc.dram_tensor(name, list(shape), dt, kind="ExternalOutput").ap()

    xT_d = din("xT", [128, 8, 768])
    cT_d = din("cT", [128, 8, 2])
    w_ada_d = din("w_ada", [4, 1024, 6144])
    w_in_d = din("w_in", [4, 1024, 3104])
    w_o_d = din("w_o", [4, 1024, 1024])
    w_up_d = din("w_up", [4, 1024, 4096])
    w_down_d = din("w_down", [4, 4096, 1024])
    badaT_d = din("badaT", [128, 4, 48])
    gvec_d = din("gvec", [128, NGV])
    wg_d = din("wg", [32, 4, 512])
    ckT_d = din("ckT", [4, 128, 4, 520])
    cv_d = din("cv", [4, 128, 4, 520])
    stf_d = din("stf", [4, 128, 2, 128])
    stb_d = din("stb", [4, 128, 2, 128])
    mask_d = din("mask", [4, 8, 128, 2048])
    cst_d = din("cst", [128, NCST])

    yT_d = dout("yT", [128, 8, 768])
    nkT_d = dout("nkT", [4, 128, 4, 512])
    nv_d = dout("nv", [4, 128, 4, 512])
    nsf_d = dout("nsf", [4, 2, 128, 2, 128])
    nsb_d = dout("nsb", [4, 2, 128, 2, 128])

    cc1_in = nc.dram_tensor("cc1_in", [128, 2048], BF16)
    cc1_out = nc.dram_tensor("cc1_out", [512, 2048], BF16)
    cc2_in = nc.dram_tensor("cc2_in", [128, 520], F32)
    cc2_out = nc.dram_tensor("cc2_out", [512, 520], F32)
    CC1I, CC1O, CC2I, CC2O = Tile("cc1i"), Tile("cc1o"), Tile("cc2i"), Tile("cc2o")

    banks = [nc.alloc_psum_tensor("pb%d" % i, [128, 512], F32) for i in range(8)]
    MM = Rot([Tile("mm0", banks[0][:, :], True), Tile("mm1", banks[1][:, :], True),
              Tile("st0", banks[2][:, :], True), Tile("st1", banks[3][:, :], True)])
    SSA = Tile("ssA", banks[6][:, 0:256], True)
    MODP_AP = banks[6][:, 256:352]
    ST = MM
    OD = Rot([Tile("od0", banks[4][:, :], True), Tile("od1", banks[5][:, :], True), Tile("od2", banks[7][:, :], True)])

    def tl(name, shape, dt):
        t = ar.alloc(name, shape, dt)
        return Tile(name, t[:] if len(shape) == 2 else t)

    x_t = ar.alloc("x", [128, 8, 768], F32)
    X = [Tile("x%d" % k, x_t[:, k, :]) for k in range(8)]
    h_t = ar.alloc("h", [128, 8, 768], BF16)
    H = [Tile("h%d" % k, h_t[:, k, :]) for k in range(8)]
    SL = Rot([tl("slab%d" % i, [128, 4160], BF16) for i in range(NSLAB)])
    WZ = tl("wz", [128, 8, 34], BF16)
    CST = tl("cst", [128, NCST], F32)
    GVEC = tl("gvec", [128, NGV], F32)
    NEGB = tl("negb", [128, 16], F32)
    BADA = tl("bada", [128, 4, 48], F32)
    WG = tl("wg", [32, 512], F32)
    CTS = tl("cts", [128, 8, 2], F32)
    SCB = tl("scb", [128, 8, 2], BF16)
    MOD = [tl("mod%d" % i, [128, 48, 2], F32) for i in range(2)]
    A1 = tl("A1", [128, 8, 2], F32)
    A2 = tl("A2", [128, 8, 2], F32)
    BLKB = tl("blkb", [128, 128], BF16)
    ONEB = tl("oneb", [128, 128], BF16)
    TRI4 = tl("tri4", [128, 512], BF16)
    ONEF = tl("onef", [128, 256], F32)
    COLS = tl("cols", [128, 4], F32)
    SQ = Rot([tl("sq%d" % i, [128, 256], BF16) for i in range(3)])
    STD = Rot([tl("std%d" % i, [128, 256], F32) for i in range(2)])
    TMPF = Rot([tl("tmpf%d" % i, [128, 256], F32) for i in range(4)])

    RL = Rot([tl("rl%d" % i, [128, 512], BF16) for i in range(2)])
    QP = [Rot([tl("qp%d_%d" % (hh_, i), [128, 256], BF16) for i in range(2)]) for hh_ in range(2)]

    class GSet:
        pass

    def mk_gset(nm):
        s = GSet()
        s.name = nm
        s.qt = [[tl("%sq%d%d" % (nm, d, c), [128, 128], BF16) for c in range(2)] for d in range(2)]
        s.kt = [[tl("%sk%d%d" % (nm, d, c), [128, 128], BF16) for c in range(2)] for d in range(2)]
        s.e = [[tl("%se%d%d" % (nm, d, c), [128, 1], F32) for c in range(2)] for d in range(2)]
        s.mid = [tl("%sm%d" % (nm, d), [128, 128], F32) for d in range(2)]
        s.fin = [tl("%sf%d" % (nm, d), [128, 128], F32) for d in range(2)]
        s.midb = [tl("%sn%d" % (nm, d), [128, 128], BF16) for d in range(2)]
        s.inib = [tl("%si%d" % (nm, d), [128, 128], BF16) for d in range(2)]
        return s

    class KSet:
        pass

    def mk_kset(nm):
        s = KSet()
        s.name = nm
        s.km0 = [[tl("%sa%d%d" % (nm, d, c), [128, 128], BF16) for c in range(2)] for d in range(2)]
        s.km1 = [[tl("%sb%d%d" % (nm, d, c), [128, 128], BF16) for c in range(2)] for d in range(2)]
        return s

    GS_P = Rot([mk_gset("gpa")])
    GS_S = [mk_gset("gs0"), mk_gset("gs1")]
    KMS = Rot([mk_kset("kma"), mk_kset("kmb")])

    mark = ar.off
    qn_t = ar.alloc("qn", [128, 4, 768], BF16)
    QN = [Tile("qn%d" % i, qn_t[:, i, :]) for i in range(4)]
    kn_t = ar.alloc("kn", [128, 4, 768], BF16)
    KN = [Tile("kn%d" % i, kn_t[:, i, :]) for i in range(4)]
    F512 = Rot([tl("f512_%d" % i, [128, 512], F32) for i in range(8)])
    KNF = F512
    GQ = [tl("gq%d" % i, [128, 768], BF16) for i in range(2)]
    GK = [tl("gk%d" % i, [128, 768], BF16) for i in range(2)]
    SG = [tl("sg%d" % i, [128, 768], BF16) for i in range(4)]
    VOWN = [tl("vown%d" % i, [128, 512], BF16) for i in range(4)]
    VS = tl("vs", [128, 2, 512], BF16)
    VST = F512
    GV = [tl("gv%d" % i, [128, 512], BF16) for i in range(6)]
    ZF = tl("zf", [32, 768], F32)
    SP = [tl("sp%d" % i, [128, 256], F32) for i in range(4)]
    CS = [tl("cs%d" % i, [128, 256], F32) for i in range(4)]
    GP = Rot([tl("gp%d" % i, [128, 4, 512], BF16) for i in range(2)])
    G2 = tl("g2", [128, 4, 520], F32)
    STG2 = tl("stg2", [128, 520], F32)
    CKB = tl("ckb", [128, 4, 520], BF16)
    CVB = tl("cvb", [128, 4, 520], BF16)
    STIN = [tl("stinf", [128, 2, 128], F32), tl("stinb", [128, 2, 128], F32)]
    MK = F512
    PT = Rot([tl("pt%d" % i, [128, 512], BF16) for i in range(3)])
    RD = Rot([tl("rd%d" % i, [128, 256], F32) for i in range(2)])
    RAW = F512
    SQB = Rot([tl("sqb%d" % i, [128, 512], BF16) for i in range(3)])
    STD5 = F512
    EX = Rot([tl("ex%d" % i, [128, 128], F32) for i in range(4)])
    KT32 = Rot([tl("kt32%d" % i, [128, 128], F32) for i in range(2)])
    AM = SQB
    BIA = Rot([tl("bia%d" % i, [128, 2], F32) for i in range(4)])

    TMPS = Rot([tl("tmps%d" % i, [128, 128], F32) for i in range(2)])
    att_end = ar.off
    ar.off = mark
    h1_t = [ar.alloc("h1_%d" % s, [128, 4, 768], BF16) for s in range(8)]
    H1 = [Tile("h1_%d" % s, h1_t[s]) for s in range(8)]
    ar.off = max(att_end, ar.off)
    DUM = tl("dum", [128, 8], F32)

    att_tiles = (QN + KN + F512.tiles + GQ + GK + SG + VOWN + [VS] + GV + [ZF] + SP + CS +
                 GP.tiles + [G2, STG2, CKB, CVB] + STIN + PT.tiles + RD.tiles +
                 SQB.tiles + EX.tiles + KT32.tiles + BIA.tiles +
                 TMPS.tiles)

    def act(out, in_, func, reads, writes, bias=None, scale=None):
        kw = {}
        if bias is not None:
            kw["bias"] = bias
        if scale is not None:
            kw["scale"] = scale
        P.add("act", lambda e: e.activation(out=out, in_=in_, func=func, **kw), reads=reads, writes=writes)

    def tt(out, in0, in1, op, reads, writes, eng="dve"):
        P.add(eng, lambda e: e.tensor_tensor(out=out, in0=in0, in1=in1, op=op), reads=reads, writes=writes)

    def ts(out, in0, s1, op0, reads, writes, s2=None, op1=None, eng="dve"):
        if op1 is None:
            P.add(eng, lambda e: e.tensor_scalar(out=out, in0=in0, scalar1=s1, scalar2=None, op0=op0),
                  reads=reads, writes=writes)
        else:
            P.add(eng, lambda e: e.tensor_scalar(out=out, in0=in0, scalar1=s1, scalar2=s2, op0=op0, op1=op1),
                  reads=reads, writes=writes)

    def stt(out, in0, scalar, in1, op0, op1, reads, writes):
        P.add("dve", lambda e: e.scalar_tensor_tensor(out=out, in0=in0, scalar=scalar, in1=in1, op0=op0, op1=op1),
              reads=reads, writes=writes)

    def cp(out, in_, reads, writes, eng="dve"):
        if eng == "act":
            act(out, in_, AF.Copy, reads, writes)
        else:
            P.add(eng, lambda e: e.tensor_copy(out=out, in_=in_), reads=reads, writes=writes)

    def recip(out, in_, reads, writes):
        P.add("dve", lambda e: e.reciprocal(out=out, in_=in_), reads=reads, writes=writes)

    def mm(out, lhsT, rhs, start, stop, reads, writes, skip=False):
        if skip:
            P.add("pe", lambda e: e.matmul(out, lhsT=lhsT, rhs=rhs, start=start, stop=stop, skip_group_check=True),
                  reads=reads, writes=writes)
        else:
            P.add("pe", lambda e: e.matmul(out, lhsT=lhsT, rhs=rhs, start=start, stop=stop), reads=reads, writes=writes)

    def rsqrt_act(T, dst_ap, src_ap, src_tiles, scale):
        act(dst_ap, src_ap, AF.Ln, src_tiles + [COLS], [T], bias=EPS, scale=scale)
        act(dst_ap, dst_ap, AF.Exp, [T], [T], scale=-0.5)

    def run_pool(jobs, k):
        jobs = iter(jobs)
        active = []
        while True:
            while len(active) < k:
                j = next(jobs, None)
                if j is None:
                    break
                active.append(j)
            if not active:
                break
            for g in list(active):
                try:
                    next(g)
                except StopIteration:
                    active.remove(g)

    def memset(ap, val, writes, eng="dve"):
        P.add(eng, lambda e: e.memset(ap, val), writes=writes)

    EPS = COLS.t[:, 0:1]
    ONE = COLS.t[:, 1:2]
    ZERO = COLS.t[:, 2:3]
    IDF = CST.t[:, C_ID:C_ID + 128]
    ROTF = CST.t[:, C_ROT:C_ROT + 128]
    COS = CST.t[:, C_COS:C_COS + 256]
    SIN = CST.t[:, C_SIN:C_SIN + 256]

    tiles_tok = [(0, 512, 0), (512, 256, 1)]

    P.dma("sp", x_t[:], xT_d, writes=X, key="xload")
    P.dma("sp", CST.t, cst_d, writes=[CST])
    P.dma("sp", GVEC.t, gvec_d, writes=[GVEC])
    P.dma("sp", BADA.t[:], badaT_d, writes=[BADA])
    P.dma("sp", CTS.t[:], cT_d, writes=[CTS])
    memset(COLS.t[:, 0:1], 1e-6, [COLS])
    memset(COLS.t[:, 1:2], 1.0, [COLS])
    memset(COLS.t[:, 2:3], 0.0, [COLS])
    memset(ONEB.t, 1.0, [ONEB])
    memset(ONEF.t, 1.0, [ONEF])
    cp(BLKB.t, CST.t[:, C_BLK:C_BLK + 128], [CST], [BLKB])
    for q_ in range(2):
        cp(TRI4.t[:, 256 * q_:256 * q_ + 256], CST.t[:, C_TRIU:C_TRIU + 256], [CST], [TRI4])
    ts(NEGB.t, GVEC.t[:, G_BF:G_BF + 16], -1.0, ALU.mult, [GVEC], [NEGB])
    act(SCB.t[:], CTS.t[:], AF.Silu, [CTS], [SCB])
    for hh_ in range(2):
        for t_ in QP[hh_].tiles:
            memset(t_.t, 0.0, [t_])
    for ks in KMS.tiles:
        for d in range(2):
            for c in range(2):
                memset(ks.km0[d][c].t, 0.0, [ks.km0[d][c]])
                memset(ks.km1[d][c].t, 0.0, [ks.km1[d][c]])

    def load_slab(src, a, b):
        sl = SL.next()
        bp = 4160 // a
        view = sl.t[:, 0:a * bp].rearrange("p (a b) -> p a b", b=bp)[:, :, 0:b]
        P.dma("pool", view, src, writes=[sl])
        return sl, view

    def kview(w_l, c0, c1):
        return w_l.rearrange("(kc p) n -> p kc n", p=128)[:, :, c0:c1]

    def ada_slab(l, s):
        sl, v = load_slab(kview(w_ada_d[l], 512 * s, 512 * s + 512), 8, 512)
        for cc in range(4):
            ch = 4 * s + cc
            for kc in range(8):
                mm(MODP_AP[:, 2 * ch:2 * ch + 2], v[:, kc, cc * 128:(cc + 1) * 128], SCB.t[:, kc, :],
                   kc == 0, kc == 7, [sl, SCB], [SSA])

    def ada_finish(l, c0=0, c1=48):
        mp = MODP_AP.rearrange("p (c v) -> p c v", v=2)
        for v_ in range(2):
            tt(MOD[l % 2].t[:, c0:c1, v_], mp[:, c0:c1, v_], BADA.t[:, l, c0:c1], ALU.add, [SSA, BADA], [MOD[l % 2]])

    def ada_lane(items, start, spacing, lead):
        slabs = []
        fins = {}
        for it in items:
            if it[0] == "slab":
                slabs.append(it)
            else:
                fins.setdefault(len(slabs) - 1, []).append(it)
        n = len(slabs)
        loaded = {}
        r = 0
        k_dma = 0
        k_use = 0
        for f in fins.get(-1, []):
            ada_finish(f[1], f[2], f[3])
        while k_use < n:
            if k_dma < n and k_dma < k_use + NSLAB and r >= start + spacing * k_dma:
                _, l_, s_ = slabs[k_dma]
                loaded[k_dma] = load_slab(kview(w_ada_d[l_], 512 * s_, 512 * s_ + 512), 8, 512)
                k_dma += 1
            if k_use < k_dma and r >= start + spacing * k_use + lead:
                _, l_, s_ = slabs[k_use]
                sl, v = loaded.pop(k_use)
                for cc in range(4):
                    ch = 4 * s_ + cc
                    for kc in range(8):
                        mm(MODP_AP[:, 2 * ch:2 * ch + 2], v[:, kc, cc * 128:(cc + 1) * 128], SCB.t[:, kc, :],
                           kc == 0, kc == 7, [sl, SCB], [SSA])
                for f in fins.get(k_use, []):
                    ada_finish(f[1], f[2], f[3])
                k_use += 1
            yield
            r += 1

    def norm(l, which):
        mod = MOD[l % 2]
        A = A1 if which == 1 else A2
        boff = 0 if which == 1 else 24
        sums = []
        for u in range(3):
            t0 = 256 * u
            ss = SSA if u == 0 else MM.next()
            for k in range(8):
                sq = SQ.next()
                if k % 2 == 0:
                    act(sq.t, X[k].t[:, t0:t0 + 256], AF.Square, [X[k]], [sq])
                else:
                    tt(sq.t, X[k].t[:, t0:t0 + 256], X[k].t[:, t0:t0 + 256], ALU.mult, [X[k]], [sq])
                mm(ss.t[:, 0:256], ONEB.t, sq.t, k == 0, k == 7, [sq, ONEB], [ss])
            sums.append(ss)
        rstds = []
        for u in range(3):
            std = STD.next() if u < 2 else TMPF.next()
            rsqrt_act(std, std.t, sums[u].t[:, 0:256], [sums[u]], 1.0 / 1024)
            rstds.append(std)
        for u in range(3):
            t0 = 256 * u
            v_ = 1 if u == 2 else 0
            rstd = rstds[u]
            for k in range(8):
                tmp = TMPF.next()
                if tmp is rstds[2]:
                    tmp = TMPF.next()
                stt(tmp.t, X[k].t[:, t0:t0 + 256], A.t[:, k, v_:v_ + 1], rstd.t, ALU.mult, ALU.mult,
                    [X[k], A, rstd], [tmp])
                if k % 4 == 3:
                    ts(H[k].t[:, t0:t0 + 256], tmp.t, mod.t[:, boff + k, v_:v_ + 1], ALU.add, [tmp, mod], [H[k]])
                else:
                    act(H[k].t[:, t0:t0 + 256], tmp.t, AF.Identity, [tmp, mod], [H[k]],
                        bias=mod.t[:, boff + k, v_:v_ + 1])

    def gates(l, u):
        t0 = 256 * u
        for ch in range(4):
            ps = MM.next()
            mm(ps.t[:, 0:256], WG.t[0:32, ch * 128:(ch + 1) * 128], ZF.t[0:32, t0:t0 + 256], True, True,
               [WG, ZF], [ps])
            e = TMPF.next()
            nb = NEGB.t[:, (0 if ch < 2 else 8) + 2 * l + (ch % 2):(0 if ch < 2 else 8) + 2 * l + (ch % 2) + 1]
            act(e.t, ps.t[:, 0:256], AF.Exp, [ps, NEGB], [e], bias=nb, scale=-1.0)
            act(SP[ch].t, e.t, AF.Ln, [e, COLS], [SP[ch]], bias=ONE)
            P.add("dve", lambda eng, o=CS[ch].t, d1=SP[ch].t: eng.tensor_tensor_scan(
                out=o, data0=ONEF.t, data1=d1, initial=0.0, op0=ALU.mult, op1=ALU.add),
                reads=[ONEF, SP[ch]], writes=[CS[ch]])
            if ch >= 2:
                tt(SP[ch].t, CS[ch].t, SP[ch].t, ALU.subtract, [CS[ch], SP[ch]], [SP[ch]])

    def gla_prep(u, pr, S, K):
        t0 = 256 * u
        for d in range(2):
            bias_, eqs, eks, k32s, psts = [], [], [], [], []
            for c in range(2):
                bia = BIA.hold()
                bias_.append(bia)
                if d == 0:
                    if c == 0:
                        memset(bia.t, 0.0, [bia])
                    else:
                        ts(bia.t[:, 0:1], CS[pr].t[:, 127:128], 1.0 / 16, ALU.mult, [CS[pr]], [bia])
                        ts(bia.t[:, 1:2], CS[pr].t[:, 127:128], -1.0 / 16, ALU.mult, [CS[pr]], [bia])
                else:
                    col = 128 * c + 127
                    ts(bia.t[:, 0:1], CS[2 + pr].t[:, col:col + 1], -1.0 / 16, ALU.mult, [CS[2 + pr]], [bia])
                    ts(bia.t[:, 1:2], CS[2 + pr].t[:, col:col + 1], 1.0 / 16, ALU.mult, [CS[2 + pr]], [bia])
            yield
            src_t = CS[pr] if d == 0 else SP[2 + pr]
            sq_, sk_ = (-1.0 / 16, 1.0 / 16) if d == 0 else (1.0 / 16, -1.0 / 16)
            for c in range(2):
                loc = slice(128 * c, 128 * c + 128)
                bia = bias_[c]
                eq = EX.hold()
                act(eq.t, src_t.t[:, loc], AF.Exp, [src_t, bia], [eq], bias=bia.t[:, 0:1], scale=sq_)
                ek = EX.hold()
                act(ek.t, src_t.t[:, loc], AF.Exp, [src_t, bia], [ek], bias=bia.t[:, 1:2], scale=sk_)
                BIA.release(bia)
                eqs.append(eq)
                eks.append(ek)
            yield
            for c in range(2):
                tok = slice(t0 + 128 * c, t0 + 128 * c + 128)
                eq, ek = eqs[c], eks[c]
                stt(S.qt[d][c].t, GQ[pr].t[:, tok], 0.125, eq.t, ALU.mult, ALU.mult, [GQ[pr], eq], [S.qt[d][c]])
                k32 = KT32.hold()
                k32s.append(k32)
                tt(k32.t, GK[pr].t[:, tok], ek.t, ALU.mult, [GK[pr], ek], [k32])
                ecol = eq.t[:, 127:128] if d == 0 else eq.t[:, 0:1]
                cp(S.e[d][c].t, ecol, [eq], [S.e[d][c]], eng="dve")
                EX.release(eq)
                EX.release(ek)
            yield
            pst = ST.hold()
            for c in range(2):
                k32 = k32s[c]
                cp(S.kt[d][c].t, k32.t, [k32], [S.kt[d][c]], eng="dve")
                P.add("pe", lambda e, o=pst.t[:, 128 * c:128 * c + 128], i=k32.t: e.transpose(o, i, IDF),
                      reads=[k32, CST], writes=[pst])
                KT32.release(k32)
            yield
            for c in range(2):
                o_ = 128 * c
                cp(K.km0[d][c].t[:, 0:64], pst.t[:, o_:o_ + 64], [pst], [K.km0[d][c]], eng="dve")
                cp(K.km1[d][c].t[:, 64:128], pst.t[:, o_ + 64:o_ + 128], [pst], [K.km1[d][c]], eng="dve")
            ST.release(pst)
            yield

    def gla_pass1(u, pr, S, K):
        def kv(pk, d, c):
            g = GV[2 * u + c]
            o_ = 128 * d
            mm(pk.t[:, o_:o_ + 128], K.km0[d][c].t, g.t[:, (2 * pr) * 128:(2 * pr) * 128 + 128], True, False,
               [K.km0[d][c], g], [pk])
            mm(pk.t[:, o_:o_ + 128], K.km1[d][c].t, g.t[:, (2 * pr + 1) * 128:(2 * pr + 1) * 128 + 128], False, True,
               [K.km1[d][c], g], [pk])
        firsts = (0, 1)
        seconds = (1, 0)
        pk = ST.hold()
        for d in range(2):
            kv(pk, d, firsts[d])
        yield
        for d in range(2):
            ts(S.mid[d].t, pk.t[:, 128 * d:128 * d + 128], S.e[d][firsts[d]].t, ALU.mult,
               [pk, S.e[d][firsts[d]]], [S.mid[d]])
        ST.release(pk)
        pk = ST.hold()
        for d in range(2):
            kv(pk, d, seconds[d])
        yield
        for d in range(2):
            tmp = TMPS.next()
            tt(tmp.t, pk.t[:, 128 * d:128 * d + 128], S.mid[d].t, ALU.add, [pk, S.mid[d]], [tmp])
            ts(S.fin[d].t, tmp.t, S.e[d][seconds[d]].t, ALU.mult, [tmp, S.e[d][seconds[d]]], [S.fin[d]])
        ST.release(pk)
        yield

    def gla_pass2(l, u, pr, S, have_init):
        t0 = 256 * u
        po = OD.hold()
        for c in range(2):
            g = GV[2 * u + c]
            pas = [ST.hold(), ST.hold()]
            for hh in range(2):
                prow = slice(hh * 64, hh * 64 + 64)
                for d in range(2):
                    o_ = 128 * d
                    mm(pas[hh].t[:, o_:o_ + 128], S.kt[d][c].t[prow, :], S.qt[d][c].t[prow, :], True, True,
                       [S.kt[d][c], S.qt[d][c]], [pas[hh]])
            yield
            am = AM.hold()
            for hh in range(2):
                tt(am.t[:, 256 * hh:256 * hh + 256], pas[hh].t[:, 0:256], TRI4.t[:, 0:256], ALU.mult,
                   [pas[hh], TRI4], [am])
                ST.release(pas[hh])
            for hh in range(2):
                h = 2 * pr + hh
                prow = slice(hh * 64, hh * 64 + 64)
                seq = []
                for d in range(2):
                    o_ = 256 * hh + 128 * d
                    seq.append((g.t[:, h * 128:h * 128 + 128], am.t[:, o_:o_ + 128], [g, am]))
                    first_chunk = (c == 0) if d == 0 else (c == 1)
                    if first_chunk:
                        if have_init:
                            seq.append((S.inib[d].t[prow, :], S.qt[d][c].t[prow, :], [S.inib[d], S.qt[d][c]]))
                    else:
                        seq.append((S.midb[d].t[prow, :], S.qt[d][c].t[prow, :], [S.midb[d], S.qt[d][c]]))
                for i, (lh, rh, rd_) in enumerate(seq):
                    o_ = 256 * hh + 128 * c
                    mm(po.t[:, o_:o_ + 128], lh, rh, i == 0, i == len(seq) - 1, rd_, [po])
            AM.release(am)
            yield
        sqo = SQB.hold()
        act(sqo.t, po.t, AF.Square, [po], [sqo])
        yield
        pss = ST.hold()
        mm(pss.t, ONEB.t, sqo.t, True, True, [sqo, ONEB], [pss])
        SQB.release(sqo)
        yield
        std = STD5.hold()
        rsqrt_act(std, std.t, pss.t, [pss], 1.0 / 128)
        ST.release(pss)
        yield
        tmp = RAW.hold()
        stt(tmp.t, po.t, GVEC.t[:, G_OUT + l:G_OUT + l + 1], std.t, ALU.mult, ALU.mult, [po, GVEC, std], [tmp])
        for hh in range(2):
            h = 2 * pr + hh
            tt(H[4 + h].t[:, t0:t0 + 256], tmp.t[:, 256 * hh:256 * hh + 256], SG[h].t[:, t0:t0 + 256], ALU.mult,
               [tmp, SG[h]], [H[4 + h]])
        RAW.release(tmp)
        STD5.release(std)
        OD.release(po)
        yield

    def gla_prompt_job(l, u, pr):
        S = GS_P.next()
        K = KMS.hold()
        yield from gla_prep(u, pr, S, K)
        yield from gla_pass1(u, pr, S, K)
        KMS.release(K)
        for d in range(2):
            cp(S.midb[d].t, S.mid[d].t, [S.mid[d]], [S.midb[d]], eng="act")
            P.dma("sp", (nsf_d if d == 0 else nsb_d)[l, u][:, pr, :], S.fin[d].t, reads=[S.fin[d]], store=True)
        yield
        yield from gla_pass2(l, u, pr, S, False)

    def na_head(l, u, pr, hh, chunks):
        t0 = 256 * u
        prow = slice(hh * 64, hh * 64 + 64)
        pod = OD.hold()
        qp = QP[hh].hold()
        cp(qp.t[prow, :], QN[pr].t[prow, t0:t0 + 256], [QN[pr]], [qp], eng="act" if hh == 0 else "dve")
        n = len(chunks)
        masked = [i0 for i0 in range(0, n, 2) if chunks[i0][4] is not None]
        mks = {}

        def prefetch(cnt):
            for i0 in masked:
                if cnt <= 0:
                    break
                if i0 not in mks:
                    ml, mh, mm_ = chunks[i0][4]
                    mk_ = MK.hold()
                    P.dma("sp" if pr % 2 == 0 else "act", mk_.t, mask_d[ml, mh][:, 256 * mm_:256 * mm_ + 512],
                          writes=[mk_])
                    mks[i0] = mk_
                    cnt -= 1

        prefetch(3)
        for i0 in range(0, n, 2):
            pair = chunks[i0:i0 + 2]
            ps = ST.hold()
            for ii, (kap, kreads, vap, vreads, msk, mreads) in enumerate(pair):
                mm(ps.t[:, 256 * ii:256 * ii + 256], kap, qp.t, True, True, kreads + [qp], [ps])
            mk = mks.get(i0)
            yield
            pt = PT.hold()
            if mk is None:
                act(pt.t, ps.t, AF.Exp, [ps], [pt], scale=0.125)
                ST.release(ps)
            else:
                stt(ps.t, ps.t, 0.125, mk.t, ALU.mult, ALU.add, [ps, mk], [ps])
                MK.release(mk)
                masked.remove(i0)
                del mks[i0]
                prefetch(3 - len(mks))
                yield
                act(pt.t, ps.t, AF.Exp, [ps], [pt])
                ST.release(ps)
            yield
            for ii, (kap, kreads, vap, vreads, msk, mreads) in enumerate(pair):
                i = i0 + ii
                mm(pod.t[:, 0:256], vap, pt.t[:, 256 * ii:256 * ii + 256], i == 0, i == n - 1, vreads + [pt], [pod], skip=True)
                mm(pod.t[:, 256:512], ONEB.t, pt.t[:, 256 * ii:256 * ii + 256], False, i == n - 1, [ONEB, pt], [pod], skip=True)
            PT.release(pt)
        yield
        rd = RD.hold()
        act(rd.t[prow, :], pod.t[prow, 256:512], AF.Ln, [pod], [rd])
        act(rd.t[prow, :], rd.t[prow, :], AF.Exp, [rd], [rd], scale=-1.0)
        yield
        tt(H[pr].t[prow, t0:t0 + 256], pod.t[prow, 0:256], rd.t[prow, :], ALU.mult, [pod, rd], [H[pr]])
        RD.release(rd)
        OD.release(pod)
        QP[hh].release(qp)
        yield

    def layer(l):
        mod = MOD[l % 2]
        for v_ in range(2):
            stt(A1.t[:, :, v_], mod.t[:, 8:16, v_], 1.0, GVEC.t[:, G_ATT + 8 * l:G_ATT + 8 * l + 8],
                ALU.add, ALU.mult, [mod, GVEC], [A1])
        norm(l, 1)
        P.dma("pool", CKB.t[:, :, 0:512], ckT_d[l][:, :, 0:512], writes=[CKB])
        P.dma("pool", CVB.t[:, :, 0:512], cv_d[l][:, :, 0:512], writes=[CVB])
        P.dma("sp", WG.t, wg_d[:, l, :], writes=[WG])
        P.dma("sp", STIN[0].t[:], stf_d[l], writes=[STIN[0]])
        P.dma("sp", STIN[1].t[:], stb_d[l], writes=[STIN[1]])
        P.add("dve", lambda e: e.memset(DUM.t[:, 0:1], 0.0), writes=[DUM] + H1 + att_tiles)

        qk_tail = []
        for s in range(6):
            sl, v = load_slab(kview(w_in_d[l], 512 * s, 512 * s + 512), 8, 512)
            if s == 2 and qk_tail:
                qk_tail.pop()()
            if s in (2, 4):
                for blk in range(6):
                    ps = MM.next()
                    for kc in range(8):
                        mm(ps.t, H[kc].t[:, blk * 128:blk * 128 + 128], v[:, kc, 0:512], kc == 0, kc == 7,
                           [sl, H[kc]], [ps])
                    if s == 2:
                        if blk < 4:
                            vst = VST.next()
                            cp(vst.t, ps.t, [ps], [vst], eng="act")
                            P.dma("sp", nv_d[l][:, blk, :], vst.t, reads=[vst], store=True)
                            cp(VOWN[blk].t, ps.t, [ps], [VOWN[blk]], eng="dve")
                        else:
                            cp(VS.t[:, blk - 4, :], ps.t, [ps], [VS], eng="act" if blk == 4 else "dve")
                    else:
                        cp(GV[blk].t, ps.t, [ps], [GV[blk]], eng="act" if blk % 2 == 0 else "dve")
                continue
            for cc in range(4):
                for (t0, n, v_) in tiles_tok:
                    ps = MM.next()
                    for kc in range(8):
                        mm(ps.t[:, 0:n], v[:, kc, cc * 128:cc * 128 + 128], H[kc].t[:, t0:t0 + n], kc == 0, kc == 7,
                           [sl, H[kc]], [ps])
                    if s in (0, 1):
                        sqb = SQB.hold()
                        act(sqb.t[:, 0:n], ps.t[:, 0:n], AF.Square, [ps], [sqb])
                        raw = RAW.hold()
                        cp(raw.t[:, 0:n], ps.t[:, 0:n], [ps], [raw], eng="dve")

                        def tail(s=s, cc=cc, t0=t0, n=n, sqb=sqb, raw=raw):
                            ps2 = MM.next()
                            mm(ps2.t[:, 0:n], BLKB.t, sqb.t[:, 0:n], True, True, [BLKB, sqb], [ps2])
                            SQB.release(sqb)
                            std = STD5.next()
                            rsqrt_act(std, std.t[:, 0:n], ps2.t[:, 0:n], [ps2], 1.0 / 64)
                            rstd = std
                            gcol = GVEC.t[:, (G_Q if s == 0 else G_K) + l:(G_Q if s == 0 else G_K) + l + 1]
                            dst = QN[cc] if s == 0 else KN[cc]
                            stt(dst.t[:, t0:t0 + n], raw.t[:, 0:n], gcol, rstd.t[:, 0:n], ALU.mult, ALU.mult,
                                [raw, GVEC, rstd], [dst])
                            if s == 1 and t0 == 0:
                                knf = KNF.next()
                                stt(knf.t, raw.t[:, 0:512], gcol, rstd.t[:, 0:512], ALU.mult, ALU.mult,
                                    [raw, GVEC, rstd], [knf])
                                P.dma("sp", nkT_d[l][:, cc, :], knf.t, reads=[knf], store=True)
                            RAW.release(raw)
                        if qk_tail:
                            qk_tail.pop()()
                        qk_tail.append(tail)
                    elif s == 3:
                        dst = GQ[cc] if cc < 2 else GK[cc - 2]
                        if v_ == 0:
                            cp(dst.t[:, t0:t0 + n], ps.t[:, 0:n], [ps], [dst], eng="act")
                        else:
                            raw = RAW.next()
                            cp(raw.t[:, 0:256], ps.t[:, 0:256], [ps], [raw], eng="act")
                            ps2 = MM.next()
                            mm(ps2.t[:, 0:256], ROTF, raw.t[:, 0:256], True, True, [CST, raw], [ps2])
                            t1 = TMPF.next()
                            tt(t1.t, raw.t[:, 0:256], COS, ALU.mult, [raw, CST], [t1])
                            t2 = TMPF.next()
                            tt(t2.t, ps2.t[:, 0:256], SIN, ALU.mult, [ps2, CST], [t2])
                            tt(dst.t[:, 512:768], t1.t, t2.t, ALU.add, [t1, t2], [dst])
                    else:
                        act(SG[cc].t[:, t0:t0 + n], ps.t[:, 0:n], AF.Silu, [ps], [SG[cc]])
        P.dma("pool", WZ.t[:, :, 0:32], kview(w_in_d[l], 3072, 3104), writes=[WZ])
        for (t0, n, v_) in tiles_tok:
            ps = MM.next()
            for kc in range(8):
                mm(ps.t[0:32, 0:n], WZ.t[:, kc, 0:32], H[kc].t[:, t0:t0 + n], kc == 0, kc == 7, [WZ, H[kc]], [ps])
            cp(ZF.t[0:32, t0:t0 + n], ps.t[0:32, 0:n], [ps], [ZF], eng="act")

        gates(l, 2)

        def sample_pre(pr):
            S = GS_S[pr]
            K = KMS.hold()
            yield from gla_prep(2, pr, S, K)
            yield from gla_pass1(2, pr, S, K)
            KMS.release(K)
            cp(STG2.t[:, pr * 256:pr * 256 + 128], S.fin[0].t, [S.fin[0]], [STG2], eng="act")
            cp(STG2.t[:, pr * 256 + 128:pr * 256 + 256], S.fin[1].t, [S.fin[1]], [STG2], eng="act")
            for d in range(2):
                tt(STG2.t[:, 512 + 2 * pr + d:512 + 2 * pr + d + 1], S.e[d][0].t, S.e[d][1].t, ALU.mult,
                   [S.e[d][0], S.e[d][1]], [STG2])
            yield

        c1v = cc1_in.ap().rearrange("p (a b) -> p a b", b=512)
        P.dma("sp", c1v[:, :, 0:256], kn_t[:, :, 512:768], reads=KN, writes=[CC1I], key="cc1w")
        for mo in range(2):
            P.dma("sp", c1v[:, :, 256 + 128 * mo:256 + 128 * mo + 128],
                  VS.t[:, mo, :].rearrange("p (a b) -> p a b", b=128), reads=[VS], writes=[CC1I], key="cc1w")
        P.add("pool", lambda e: e.collective_compute(
            "AllGather", ALU.bypass, replica_groups=[[0, 1, 2, 3], [4, 5, 6, 7]],
            ins=[cc1_in.ap().opt()], outs=[cc1_out.ap().opt()]),
            reads=[CC1I], writes=[CC1O], key="cc", inc=1)
        memset(STG2.t[:, 516:520], 0.0, [STG2])

        OH = CST.t[:, C_OH:C_OH + 4]

        g2_state = {"loaded": False}

        def ensure_g2():
            if not g2_state["loaded"]:
                P.dma("sp", G2.t[:], cc2_out.ap().rearrange("(j p) x -> p j x", p=128), reads=[CC2O], writes=[G2])
                g2_state["loaded"] = True

        def fold_job(pr):
            ensure_g2()
            S = GS_S[pr]
            W = F512.hold()
            for d in range(2):
                ini = W.t[:, 128 * d:128 * d + 128]
                blk = [W.t[:, 256:384], W.t[:, 384:512]]
                cur = blk[0]
                cp(cur, STIN[d].t[:, pr, :], [STIN[d]], [W], eng="act")
                order = [0, 1, 2, 3] if d == 0 else [3, 2, 1, 0]
                ts(ini, cur, OH[:, order[0]:order[0] + 1], ALU.mult, [W, CST], [W])
                for idx in range(3):
                    j = order[idx]
                    nxt = blk[(idx + 1) % 2]
                    dcol = G2.t[:, j, 512 + 2 * pr + d:512 + 2 * pr + d + 1]
                    acol = G2.t[:, j, pr * 256 + 128 * d:pr * 256 + 128 * d + 128]
                    stt(nxt, cur, dcol, acol, ALU.mult, ALU.add, [W, G2], [W])
                    jn = order[idx + 1]
                    stt(ini, nxt, OH[:, jn:jn + 1], ini, ALU.mult, ALU.add, [W, CST], [W])
                    cur = nxt
                    yield
                cp(S.inib[d].t, ini, [W], [S.inib[d]], eng="act")
                c_first = 0 if d == 0 else 1
                stt(S.midb[d].t, ini, S.e[d][c_first].t, S.mid[d].t, ALU.mult, ALU.add,
                    [W, S.e[d][c_first], S.mid[d]], [S.midb[d]])
                yield
            F512.release(W)
            yield from gla_pass2(l, 2, pr, S, True)

        def gla_lane():
            for pr in range(2):
                yield from sample_pre(pr)
            P.dma("sp", cc2_in.ap(), STG2.t, reads=[STG2], writes=[CC2I], key="cc2w")
            P.add("pool", lambda e: e.collective_compute(
                "AllGather", ALU.bypass, replica_groups=[[0, 1, 2, 3], [4, 5, 6, 7]],
                ins=[cc2_in.ap().opt()], outs=[cc2_out.ap().opt()]),
                reads=[CC2I], writes=[CC2O], key="cc", inc=1)
            yield
            for u in range(2):
                gates(l, u)
                yield
                for pr in range(2):
                    yield from gla_prompt_job(l, u, pr)
            yield from fold_job(0)

        g1v = cc1_out.ap().rearrange("(j p) x -> p j x", p=128)
        gp_of = {}

        def sample_head(pr, hh):
            if hh == 0:
                gp = GP.hold()
                gp_of[pr] = gp
                P.dma("pool", gp.t[:], g1v[:, :, pr * 512:pr * 512 + 512], reads=[CC1O], writes=[gp])
            gp = gp_of[pr]
            h = 2 * pr + hh
            prow = slice(hh * 64, hh * 64 + 64)
            chunks = []
            for kc in range(4):
                chunks.append((CKB.t[:, pr, 128 * kc:128 * kc + 128], [CKB],
                               CVB.t[:, kc, 128 * pr:128 * pr + 128], [CVB], None, []))
            for m in range(8):
                j, mo = m // 2, m % 2
                chunks.append((gp.t[:, j, 128 * mo:128 * mo + 128], [gp],
                               gp.t[:, j, 256 + 128 * mo:256 + 128 * mo + 128], [gp],
                               (l, h, m), None))
            yield from na_head(l, 2, pr, hh, chunks)
            if hh == 1:
                GP.release(gp)

        items = []
        if l == 0:
            items += [("slab", 0, s_) for s_ in range(4, 12)] + [("fin", 0, 16, 48)]
        if l + 1 < n_layers:
            items += [("slab", l + 1, s_) for s_ in range(12)] + [("fin", l + 1, 0, 48)]
        lane_done = {}

        def na_lane(prs):
            for u in range(2):
                t0 = 256 * u
                for pr in prs:
                    for hh in range(2):
                        prow = slice(hh * 64, hh * 64 + 64)
                        chunks = []
                        for m in range(2):
                            chunks.append((KN[pr].t[:, t0 + 128 * m:t0 + 128 * m + 128], [KN[pr]],
                                           VOWN[2 * u + m].t[:, pr * 128:pr * 128 + 128], [VOWN[2 * u + m]], None, []))
                        yield from na_head(l, u, pr, hh, chunks)
            for pr in prs:
                for hh in range(2):
                    yield from sample_head(pr, hh)
            lane_done[prs] = True
            if prs == (1, 3):
                while not lane_done.get((0, 2)):
                    yield
                yield from fold_job(1)

        run_pool([gla_lane(), ada_lane(items, 16, 9 if l > 0 else 5, 24 if l > 0 else 12), na_lane((0, 2)), na_lane((1, 3))], 4)

        for s in range(2):
            sl, v = load_slab(kview(w_o_d[l], 512 * s, 512 * s + 512), 8, 512)
            for cc in range(4):
                ch = 4 * s + cc
                for (t0, n, v_) in tiles_tok:
                    ps = MM.next()
                    for kc in range(8):
                        mm(ps.t[:, 0:n], v[:, kc, cc * 128:cc * 128 + 128], H[kc].t[:, t0:t0 + n], kc == 0, kc == 7,
                           [sl, H[kc]], [ps])
                    stt(X[ch].t[:, t0:t0 + n], ps.t[:, 0:n], mod.t[:, 16 + ch, v_:v_ + 1], X[ch].t[:, t0:t0 + n],
                        ALU.mult, ALU.add, [ps, mod, X[ch]], [X[ch]])

        for v_ in range(2):
            stt(A2.t[:, :, v_], mod.t[:, 32:40, v_], 1.0, GVEC.t[:, G_MLP + 8 * l:G_MLP + 8 * l + 8],
                ALU.add, ALU.mult, [mod, GVEC], [A2])
        norm(l, 2)
        P.add("dve", lambda e: e.memset(DUM.t[:, 1:2], 0.0), writes=[DUM] + H1 + att_tiles)
        for s in range(8):
            sl, v = load_slab(kview(w_up_d[l], 512 * s, 512 * s + 512), 8, 512)
            for cc in range(4):
                for (t0, n, v_) in tiles_tok:
                    ps = MM.next()
                    for kc in range(8):
                        mm(ps.t[:, 0:n], v[:, kc, cc * 128:cc * 128 + 128], H[kc].t[:, t0:t0 + n], kc == 0, kc == 7,
                           [sl, H[kc]], [ps])
                    r = RL.next()
                    act(r.t[:, 0:n], ps.t[:, 0:n], AF.Relu, [ps], [r])
                    tt(H1[s].t[:, cc, t0:t0 + n], r.t[:, 0:n], r.t[:, 0:n], ALU.mult, [r], [H1[s]])
        for mc in range(8):
            src = w_down_d[l].rearrange("(kc p) n -> p kc n", p=128)[:, :, mc * 128:mc * 128 + 128]
            sl, v = load_slab(src, 32, 128)
            for (t0, n, v_) in tiles_tok:
                ps = MM.next()
                for kc in range(32):
                    mm(ps.t[:, 0:n], v[:, kc, 0:128], H1[kc // 4].t[:, kc % 4, t0:t0 + n], kc == 0, kc == 31,
                       [sl, H1[kc // 4]], [ps])
                stt(X[mc].t[:, t0:t0 + n], ps.t[:, 0:n], mod.t[:, 40 + mc, v_:v_ + 1], X[mc].t[:, t0:t0 + n],
                    ALU.mult, ALU.add, [ps, mod, X[mc]], [X[mc]])

    for s in range(4):
        ada_slab(0, s)
    ada_finish(0, 0, 16)
    for l in range(n_layers):
        layer(l)
    P.dma("sp", yT_d, x_t[:], reads=X, store=True, key="ystore")
    P.emit(nc)
    return nc, P.stats + " sbuf_peak=%d" % ar.peak


def _consts(j):
    cst = np.zeros((128, NCST), np.float32)
    cst[:, C_ID:C_ID + 128] = np.eye(128, dtype=np.float32)
    blk = np.zeros((128, 128), np.float32)
    blk[:64, :64] = 1.0
    blk[64:, 64:] = 1.0
    cst[:, C_BLK:C_BLK + 128] = blk
    rot = np.zeros((128, 128), np.float32)
    for m in range(128):
        part = (m % 32) // 16
        if part == 0:
            rot[m + 16, m] = -1.0
        else:
            rot[m - 16, m] = 1.0
    cst[:, C_ROT:C_ROT + 128] = rot
    jj = np.arange(128)[:, None]
    ii = np.arange(128)[None, :]
    cst[:, C_TRIU:C_TRIU + 128] = (ii >= jj).astype(np.float32)
    cst[:, C_TRIL:C_TRIL + 128] = (ii <= jj).astype(np.float32)
    inv = (np.float32(10000.0) ** (-np.arange(16, dtype=np.float32) / np.float32(16))).astype(np.float32)
    t = 256 * j + np.arange(256)
    row = (t // 64).astype(np.float32)
    col = (t % 64).astype(np.float32)
    for p in range(128):
        dim = p % 64
        half = dim // 32
        i = (dim % 32) % 16
        ang = ((row if half == 0 else col) * inv[i]).astype(np.float32)
        cst[p, C_COS:C_COS + 256] = np.cos(ang).astype(np.float32)
        cst[p, C_SIN:C_SIN + 256] = np.sin(ang).astype(np.float32)
    cst[:, C_OH + j] = 1.0
    return cst


def _mask(rpb, j):
    out = np.full((4, 8, 128, 8, 256), NEG, np.float32)
    qc = np.arange(64)
    cs = np.clip(qc - 8, 0, 48)
    kc = np.arange(64)
    valid = (kc[:, None] >= cs[None, :]) & (kc[:, None] <= cs[None, :] + 15)
    dc = np.clip(kc[:, None] - qc[None, :] + 15, 0, 30)
    for rl in range(4):
        r = 4 * j + rl
        rs = min(max(r - 4, 0), 8)
        for m in range(8):
            for par in range(2):
                kr = 2 * m + par
                if not (rs <= kr <= rs + 7):
                    continue
                dr = kr - r + 7
                vals = rpb[:, :, dr][:, :, dc]
                blk = np.where(valid[None, None], vals, np.float32(NEG))
                out[:, :, par * 64:(par + 1) * 64, m, rl * 64:(rl + 1) * 64] = blk
    return np.ascontiguousarray(out.reshape(4, 8, 128, 2048))


_CACHE = {}


def kernel(x_prompt, x_sample, cache_k, cache_v, state_fwd, state_bwd, c, c_ctx,
           w_ada, b_ada, g_attn, w_in, g_q, g_k, rpb, w_gf, b_gf, w_gb, b_gb,
           g_gla_out, w_o, g_mlp, w_up, w_down):
    f = lambda a: np.ascontiguousarray(np.asarray(a, dtype=np.float32))
    x_prompt, x_sample, cache_k, cache_v = f(x_prompt), f(x_sample), f(cache_k), f(cache_v)
    state_fwd, state_bwd, c, c_ctx = f(state_fwd), f(state_bwd), f(c), f(c_ctx)
    w_ada, w_in, w_o, w_up, w_down = f(w_ada), f(w_in), f(w_o), f(w_up), f(w_down)
    b_ada, g_attn, g_q, g_k, rpb = f(b_ada), f(g_attn), f(g_q), f(g_k), f(rpb)
    w_gf, b_gf, w_gb, b_gb, g_gla_out, g_mlp = f(w_gf), f(b_gf), f(w_gb), f(b_gb), f(g_gla_out), f(g_mlp)

    badaT = np.ascontiguousarray(b_ada.reshape(4, 48, 128).transpose(2, 0, 1))
    gvec = np.zeros((128, NGV), np.float32)
    gvec[:, G_ATT:G_ATT + 32] = g_attn.reshape(4, 8, 128).transpose(2, 0, 1).reshape(128, 32)
    gvec[:, G_MLP:G_MLP + 32] = g_mlp.reshape(4, 8, 128).transpose(2, 0, 1).reshape(128, 32)
    gvec[:, G_Q:G_Q + 4] = np.tile(g_q.T, (2, 1))
    gvec[:, G_K:G_K + 4] = np.tile(g_k.T, (2, 1))
    gvec[:, G_OUT:G_OUT + 4] = g_gla_out.T
    gvec[:, G_BF:G_BF + 8] = b_gf.reshape(4, 2, 128).transpose(2, 0, 1).reshape(128, 8)
    gvec[:, G_BB:G_BB + 8] = b_gb.reshape(4, 2, 128).transpose(2, 0, 1).reshape(128, 8)
    wg = np.zeros((32, 4, 512), np.float32)
    wg[0:16, :, 0:256] = w_gf.transpose(1, 0, 2)
    wg[16:32, :, 256:512] = w_gb.transpose(1, 0, 2)

    per_b = []
    for b in range(2):
        ckT = np.zeros((4, 128, 4, 520), np.float32)
        ckT[:, :, :, 0:512] = cache_k[b].reshape(4, 4, 2, 512, 64).transpose(0, 2, 4, 1, 3).reshape(4, 128, 4, 512)
        cv = np.zeros((4, 128, 4, 520), np.float32)
        cv[:, :, :, 0:512] = cache_v[b].reshape(4, 8, 4, 128, 64).transpose(0, 3, 2, 1, 4).reshape(4, 128, 4, 512)
        stf = np.ascontiguousarray(state_fwd[b].reshape(4, 2, 2, 64, 128).transpose(0, 2, 3, 1, 4).reshape(4, 128, 2, 128))
        stb = np.ascontiguousarray(state_bwd[b].reshape(4, 2, 2, 64, 128).transpose(0, 2, 3, 1, 4).reshape(4, 128, 2, 128))
        per_b.append((ckT, cv, stf, stb))
    masks = [_mask(rpb, j) for j in range(4)]
    csts = [_consts(j) for j in range(4)]

    in_maps = []
    for core in range(8):
        b, j = core // 4, core % 4
        xs = np.concatenate([x_prompt[2 * core], x_prompt[2 * core + 1], x_sample[b, 256 * j:256 * j + 256]], axis=0)
        xT = np.ascontiguousarray(xs.reshape(768, 8, 128).transpose(2, 1, 0))
        cT = np.ascontiguousarray(np.stack([c_ctx, c[b]], axis=-1).reshape(8, 128, 2).transpose(1, 0, 2))
        ckT, cv, stf, stb = per_b[b]
        in_maps.append({
            "xT": xT, "cT": cT, "w_ada": w_ada, "w_in": w_in, "w_o": w_o, "w_up": w_up, "w_down": w_down,
            "badaT": badaT, "gvec": gvec, "wg": wg, "ckT": ckT, "cv": cv, "stf": stf, "stb": stb,
            "mask": masks[j], "cst": csts[j],
        })

    if "nc" not in _CACHE:
        _CACHE["nc"], _CACHE["stats"] = build_program(4)
    nc = _CACHE["nc"]
    res = run_bass_kernel_spmd(nc, in_maps, core_ids=list(range(8)))
    R = res.results

    y_prompt = np.zeros((16, 256, 1024), np.float32)
    y_sample = np.zeros((2, 1024, 1024), np.float32)
    new_k = np.zeros((16, 4, 8, 256, 64), np.float32)
    new_v = np.zeros((16, 4, 8, 256, 64), np.float32)
    new_sf = np.zeros((16, 4, 4, 64, 128), np.float32)
    new_sb = np.zeros((16, 4, 4, 64, 128), np.float32)
    for core in range(8):
        b, j = core // 4, core % 4
        r = R[core]
        y = np.asarray(r["yT"]).transpose(2, 1, 0).reshape(768, 1024)
        y_prompt[2 * core] = y[0:256]
        y_prompt[2 * core + 1] = y[256:512]
        y_sample[b, 256 * j:256 * j + 256] = y[512:768]
        nk = np.asarray(r["nkT"]).reshape(4, 2, 64, 4, 2, 256)
        new_k[2 * core:2 * core + 2] = nk.transpose(4, 0, 3, 1, 5, 2).reshape(2, 4, 8, 256, 64)
        nv = np.asarray(r["nv"]).reshape(4, 128, 2, 2, 8, 64)
        new_v[2 * core:2 * core + 2] = nv.transpose(2, 0, 4, 3, 1, 5).reshape(2, 4, 8, 256, 64)
        for name, dst in (("nsf", new_sf), ("nsb", new_sb)):
            s_ = np.asarray(r[name]).reshape(4, 2, 2, 64, 2, 128)
            dst[2 * core:2 * core + 2] = s_.transpose(1, 0, 4, 2, 3, 5).reshape(2, 4, 4, 64, 128)
    return (y_prompt, y_sample, new_k, new_v, new_sf, new_sb)
```
